# Optimizing a Trainium2 kernel written in Bass

```python
import math
import jax, jax.numpy as jnp
from jax import lax
import numpy as np

D_MODEL = 2048
BATCH = 8
SEQ = 4096
DEPTH = 2

GRID_W = 64
CTX_LEN = 256
Q_BLOCK = 128
ROPE_THETA = 10000.0
EPS = 1e-6
ALPHA = (2 * DEPTH) ** 0.25
BETA = (8 * DEPTH) ** -0.25
N_EVEN = (DEPTH + 1) // 2
N_ODD = DEPTH // 2

CONV_WIDTH = D_MODEL // 2
CONV_SIZE = 3
MLA_HEADS = D_MODEL // 256
MLA_Q_RANK = 512
MLA_KV_RANK = 512
MLA_NOPE = 128
MLA_ROPE = 64
MLA_V = 128
MIX_EVEN = CONV_WIDTH + MLA_HEADS * MLA_V
EVEN_IN = 3 * CONV_WIDTH + MLA_Q_RANK + MLA_KV_RANK + MLA_ROPE + MIX_EVEN
GQA_HEAD_DIM = 128
GQA_HEADS = D_MODEL // GQA_HEAD_DIM
GQA_KV_HEADS = GQA_HEADS // 4
GQA_GROUP = GQA_HEADS // GQA_KV_HEADS
MIX_ODD = GQA_HEADS * GQA_HEAD_DIM
ODD_IN = MIX_ODD + 2 * GQA_KV_HEADS * GQA_HEAD_DIM + MIX_ODD

kernel_name = 'hybrid_conv_mla_gqa_prefix_deepnorm'


def _rms(x, g):
    xf = x.astype(jnp.float32)
    y = xf * lax.rsqrt(jnp.mean(xf * xf, axis=-1, keepdims=True) + EPS)
    return (y * g.astype(jnp.float32)).astype(x.dtype)


def _post_norm(x, y, g, b):
    z = ALPHA * x.astype(jnp.float32) + y.astype(jnp.float32)
    mu = jnp.mean(z, axis=-1, keepdims=True)
    var = jnp.mean(jnp.square(z - mu), axis=-1, keepdims=True)
    zn = (z - mu) * lax.rsqrt(var + EPS)
    return (zn * g.astype(jnp.float32) + b.astype(jnp.float32)).astype(x.dtype)


def _rope_1d(x, pos):
    half = x.shape[-1] // 2
    inv = ROPE_THETA ** (-jnp.arange(half, dtype=jnp.float32) / half)
    ang = pos[:, None] * inv[None, :]
    cos = jnp.cos(ang)[:, None, :]
    sin = jnp.sin(ang)[:, None, :]
    xf = x.astype(jnp.float32)
    x1, x2 = xf[..., :half], xf[..., half:]
    return jnp.concatenate([x1 * cos - x2 * sin, x1 * sin + x2 * cos], axis=-1).astype(x.dtype)


def _rope_2d(x, pos_row, pos_col):
    d = x.shape[-1] // 2
    return jnp.concatenate([_rope_1d(x[..., :d], pos_row), _rope_1d(x[..., d:], pos_col)], axis=-1)


def _short_conv(u, w):
    up = jnp.pad(u, ((0, 0), (1, 1), (0, 0)))
    return up[:, :-2] * w[0] + up[:, 1:-1] * w[1] + up[:, 2:] * w[2]


def _attend(q, k, v):
    bsz, n, hk, g, dq = q.shape
    nb = n // Q_BLOCK
    scale = dq ** -0.5
    qb = q.reshape(bsz, nb, Q_BLOCK, hk, g, dq).transpose(1, 0, 2, 3, 4, 5)

    def one_block(qi):
        s = jnp.einsum('bqhgd,bkhd->bhgqk', qi, k).astype(jnp.float32) * scale
        p = jax.nn.softmax(s, axis=-1).astype(v.dtype)
        return jnp.einsum('bhgqk,bkhd->bqhgd', p, v)

    o = lax.map(one_block, qb)
    return o.transpose(1, 0, 2, 3, 4, 5).reshape(bsz, n, hk * g * v.shape[-1])


def _even_mixer(h, hc, w_in, conv_w, q_norm, w_qb, kv_norm, w_kvb, w_out, pos, need_ctx):
    o_c = 3 * CONV_WIDTH
    o_q = o_c + MLA_Q_RANK
    o_kv = o_q + MLA_KV_RANK + MLA_ROPE

    def kv_heads(kvp, rpos):
        bsz, m, _ = kvp.shape
        kv = (_rms(kvp[..., :MLA_KV_RANK], kv_norm) @ w_kvb).reshape(bsz, m, MLA_HEADS, MLA_NOPE + MLA_V)
        k_rope = kvp[..., MLA_KV_RANK:][:, :, None, :]
        if rpos is not None:
            k_rope = _rope_2d(k_rope, *rpos)
        k_rope = jnp.broadcast_to(k_rope, (bsz, m, MLA_HEADS, MLA_ROPE))
        return jnp.concatenate([kv[..., :MLA_NOPE], k_rope], axis=-1), kv[..., MLA_NOPE:]

    def q_heads(qp, rpos):
        bsz, m, _ = qp.shape
        q = (_rms(qp, q_norm) @ w_qb).reshape(bsz, m, MLA_HEADS, MLA_NOPE + MLA_ROPE)
        q_rope = q[..., MLA_NOPE:]
        if rpos is not None:
            q_rope = _rope_2d(q_rope, *rpos)
        return jnp.concatenate([q[..., :MLA_NOPE], q_rope], axis=-1)[:, :, :, None, :]

    def branch(p, attn):
        cb = p[..., :CONV_WIDTH]
        cc = p[..., CONV_WIDTH:2 * CONV_WIDTH]
        cu = p[..., 2 * CONV_WIDTH:o_c]
        conv = cb * _short_conv(cc * cu, conv_w)
        gate = jax.nn.silu(p[..., o_kv:])
        return (jnp.concatenate([conv, attn], axis=-1) * gate) @ w_out

    p = h @ w_in
    k_l, v_l = kv_heads(p[..., o_q:o_kv], pos)
    if need_ctx:
        pc = hc @ w_in
        kvc = pc[..., o_q:o_kv]
    else:
        kvc = hc @ w_in[:, o_q:o_kv]
    k_c, v_c = kv_heads(kvc, None)
    k = jnp.concatenate([k_l, k_c], axis=1)
    v = jnp.concatenate([v_l, v_c], axis=1)
    y = branch(p, _attend(q_heads(p[..., o_c:o_q], pos), k, v))
    y_ctx = branch(pc, _attend(q_heads(pc[..., o_c:o_q], None), k_c, v_c)) if need_ctx else None
    return y, y_ctx


def _odd_mixer(h, hc, w_in, q_norm, k_norm, w_out, pos, need_ctx):
    o_q = MIX_ODD
    o_k = o_q + GQA_KV_HEADS * GQA_HEAD_DIM
    o_v = o_k + GQA_KV_HEADS * GQA_HEAD_DIM

    def kv_of(kvp, rpos):
        bsz, m, _ = kvp.shape
        k = _rms(kvp[..., :o_k - o_q].reshape(bsz, m, GQA_KV_HEADS, GQA_HEAD_DIM), k_norm)
        v = kvp[..., o_k - o_q:].reshape(bsz, m, GQA_KV_HEADS, GQA_HEAD_DIM)
        if rpos is not None:
            k = _rope_2d(k, *rpos)
        return k, v

    def q_of(qp, rpos):
        bsz, m, _ = qp.shape
        q = _rms(qp.reshape(bsz, m, GQA_HEADS, GQA_HEAD_DIM), q_norm)
        if rpos is not None:
            q = _rope_2d(q, *rpos)
        return q.reshape(bsz, m, GQA_KV_HEADS, GQA_GROUP, GQA_HEAD_DIM)

    def branch(p, attn):
        return (attn * jax.nn.silu(p[..., o_v:])) @ w_out

    p = h @ w_in
    k_l, v_l = kv_of(p[..., o_q:o_v], pos)
    if need_ctx:
        pc = hc @ w_in
        kvc = pc[..., o_q:o_v]
    else:
        kvc = hc @ w_in[:, o_q:o_v]
    k_c, v_c = kv_of(kvc, None)
    k = jnp.concatenate([k_l, k_c], axis=1)
    v = jnp.concatenate([v_l, v_c], axis=1)
    y = branch(p, _attend(q_of(p[..., :o_q], pos), k, v))
    y_ctx = branch(pc, _attend(q_of(pc[..., :o_q], None), k_c, v_c)) if need_ctx else None
    return y, y_ctx


def setup_inputs(seed: int = 0) -> dict:
    key = jax.random.key(seed)
    ks = jax.random.split(key, 24)
    f32 = jnp.float32

    def nrm(k, shape, scale):
        return jax.random.normal(k, shape, f32) * scale

    def gain(k, shape):
        return 1.0 + 0.02 * jax.random.normal(k, shape, f32)

    return {
        'x': nrm(ks[0], (BATCH, SEQ, D_MODEL), 1.0),
        'c': nrm(ks[1], (BATCH, D_MODEL), 1.0),
        'ctx': nrm(ks[2], (BATCH, CTX_LEN, D_MODEL), 1.0),
        'c_ctx': nrm(ks[3], (D_MODEL,), 1.0),
        'w_mod': nrm(ks[4], (DEPTH, D_MODEL, 3 * D_MODEL), 0.5 * D_MODEL ** -0.5),
        'b_mod': nrm(ks[5], (DEPTH, 3 * D_MODEL), 0.01),
        'ln_g': gain(ks[6], (DEPTH, D_MODEL)),
        'ln_b': nrm(ks[7], (DEPTH, D_MODEL), 0.02),
        'a_w_in': nrm(ks[8], (N_EVEN, D_MODEL, EVEN_IN), D_MODEL ** -0.5),
        'a_conv_w': nrm(ks[9], (N_EVEN, CONV_SIZE, CONV_WIDTH), CONV_SIZE ** -0.5),
        'a_q_norm': gain(ks[10], (N_EVEN, MLA_Q_RANK)),
        'a_w_qb': nrm(ks[11], (N_EVEN, MLA_Q_RANK, MLA_HEADS * (MLA_NOPE + MLA_ROPE)), MLA_Q_RANK ** -0.5),
        'a_kv_norm': gain(ks[12], (N_EVEN, MLA_KV_RANK)),
        'a_w_kvb': nrm(ks[13], (N_EVEN, MLA_KV_RANK, MLA_HEADS * (MLA_NOPE + MLA_V)), MLA_KV_RANK ** -0.5),
        'a_w_out': nrm(ks[14], (N_EVEN, MIX_EVEN, D_MODEL), BETA * MIX_EVEN ** -0.5),
        'c_w_in': nrm(ks[15], (N_ODD, D_MODEL, ODD_IN), D_MODEL ** -0.5),
        'c_q_norm': gain(ks[16], (N_ODD, GQA_HEAD_DIM)),
        'c_k_norm': gain(ks[17], (N_ODD, GQA_HEAD_DIM)),
        'c_w_out': nrm(ks[18], (N_ODD, MIX_ODD, D_MODEL), BETA * MIX_ODD ** -0.5),
    }


def reference(x, c, ctx, c_ctx, w_mod, b_mod, ln_g, ln_b, a_w_in, a_conv_w, a_q_norm, a_w_qb,
              a_kv_norm, a_w_kvb, a_w_out, c_w_in, c_q_norm, c_k_norm, c_w_out):
    n = x.shape[1]
    rows = n // GRID_W
    pos_row = jnp.broadcast_to(jnp.arange(rows)[:, None], (rows, GRID_W)).reshape(-1).astype(jnp.float32)
    pos_col = jnp.broadcast_to(jnp.arange(GRID_W)[None, :], (rows, GRID_W)).reshape(-1).astype(jnp.float32)
    pos = (pos_row, pos_col)

    s_lat = jax.nn.silu(c)
    s_ctx = jax.nn.silu(c_ctx)
    h_ctx = ctx
    for layer in range(DEPTH):
        last = layer == DEPTH - 1
        shift, scale, gate = jnp.split(s_lat @ w_mod[layer] + b_mod[layer], 3, axis=-1)
        shift_c, scale_c, gate_c = jnp.split(s_ctx @ w_mod[layer] + b_mod[layer], 3, axis=-1)
        x_in = x * (1.0 + scale[:, None, :]) + shift[:, None, :]
        c_in = h_ctx * (1.0 + scale_c) + shift_c
        if layer % 2 == 0:
            i = layer // 2
            y, y_ctx = _even_mixer(x_in, c_in, a_w_in[i], a_conv_w[i], a_q_norm[i], a_w_qb[i],
                                   a_kv_norm[i], a_w_kvb[i], a_w_out[i], pos, not last)
        else:
            i = layer // 2
            y, y_ctx = _odd_mixer(x_in, c_in, c_w_in[i], c_q_norm[i], c_k_norm[i], c_w_out[i], pos, not last)
        x = _post_norm(x, gate[:, None, :] * y, ln_g[layer], ln_b[layer])
        if not last:
            h_ctx = _post_norm(h_ctx, gate_c * y_ctx, ln_g[layer], ln_b[layer])
    return x
```

```python
import numpy as np
import concourse.bass as bass
import concourse.mybir as mybir
from concourse.bass_utils import run_bass_kernel_spmd

F32 = mybir.dt.float32
BF16 = mybir.dt.bfloat16
ALU = mybir.AluOpType
AF = mybir.ActivationFunctionType
AX = mybir.AxisListType


class Buf:
    __slots__ = ("writers", "readers", "name")

    def __init__(self, name=""):
        self.writers = {}
        self.readers = {}
        self.name = name


class _Eng:
    def __init__(self, name):
        self.name = name
        self.ops = []
        self.sems = []
        self.count = 0
        self.waited = {}


class Prog:
    EPOCH = 30000
    SAME_ENG_SYNC = True

    def __init__(self, nc, n_dma_slots=16):
        self.nc = nc
        self.engs = {n: _Eng(n) for n in ("pe", "act", "dve", "pool", "sp")}
        self.dma_slots = {}
        self.dma_rr = {}
        self.n_dma_slots = n_dma_slots
        self.semkeys = {}

    def _merge(self, d, src):
        for k, v in src.items():
            if d.get(k, 0) < v:
                d[k] = v

    def _deps(self, reads, writes, partial):
        deps = {}
        for b in reads:
            self._merge(deps, b.writers)
        for b in writes:
            self._merge(deps, b.readers)
            if not partial:
                self._merge(deps, b.writers)
        return deps

    def _waits(self, eng, deps):
        waits = []
        for sem, val in deps.items():
            owner = self.semkeys.get(id(sem))
            if owner == eng.name and (eng.name == "pe" or not self.SAME_ENG_SYNC):
                continue
            if eng.waited.get(id(sem), 0) < val:
                eng.waited[id(sem)] = val
                waits.append((sem, val))
        return waits

    def _record(self, token, reads, writes, partial):
        sem, val = token
        for b in reads:
            if b.readers.get(sem, 0) < val:
                b.readers[sem] = val
        for b in writes:
            if b.writers.get(sem, 0) < val:
                b.writers[sem] = val

    def op(self, engname, fn, reads=(), writes=(), signal=True, partial=False, late_reads=(), late_writes=()):
        eng = self.engs[engname]
        waits = self._waits(eng, self._deps(reads, writes, partial))
        lwaits = []
        if late_reads or late_writes:
            lwaits = self._waits(eng, self._deps(late_reads, late_writes, partial))
            reads = list(reads) + list(late_reads)
            writes = list(writes) + list(late_writes)
            waits = waits + lwaits[:-1]
            lwaits = lwaits[-1:]
        token = None
        if signal:
            idx = eng.count // self.EPOCH
            while len(eng.sems) <= idx:
                s = self.nc.alloc_semaphore(f"c_{eng.name}_{len(eng.sems)}")
                self.semkeys[id(s)] = eng.name
                eng.sems.append(s)
            sem = eng.sems[idx]
            val = eng.count % self.EPOCH + 1
            eng.count += 1
            token = (sem, val)
            self._record(token, reads, writes, partial)
        eng.ops.append((waits, fn, token, 1, lwaits))
        return token

    def dma(self, queue, out, in_, reads=(), writes=(), partial=False, **kw):
        eng = self.engs[queue]
        if queue not in self.dma_slots:
            self.dma_slots[queue] = []
            self.dma_rr[queue] = 0
        slots = self.dma_slots[queue]
        rr = self.dma_rr[queue]
        self.dma_rr[queue] = rr + 1
        if len(slots) < self.n_dma_slots:
            s = self.nc.alloc_semaphore(f"d_{queue}_{len(slots)}")
            self.semkeys[id(s)] = "dma"
            slots.append([s, 0])
        slot = slots[rr % self.n_dma_slots]
        deps = self._deps(reads, writes, partial)
        if slot[1] > 0:
            self._merge(deps, {slot[0]: 16 * slot[1]})
        waits = self._waits(eng, deps)
        slot[1] += 1
        token = (slot[0], 16 * slot[1])
        self._record(token, reads, writes, partial)

        def fn(e, out=out, in_=in_, kw=kw):
            return e.dma_start(out=out, in_=in_, **kw)

        eng.ops.append((waits, fn, token, 16, []))
        return token

    def finish(self):
        eng = self.engs["sp"]
        deps = {}
        for q, slots in self.dma_slots.items():
            for s, n in slots:
                if n:
                    deps[s] = 16 * n
        for n, e in self.engs.items():
            if e.count:
                idx = (e.count - 1) // self.EPOCH
                deps[e.sems[idx]] = (e.count - 1) % self.EPOCH + 1
        waits = self._waits(eng, deps)
        eng.ops.append((waits, None, None, 0, []))

    def emit(self):
        with self.nc.Block() as block:
            def mk(engname):
                ops = self.engs[engname].ops

                def body(e):
                    for waits, fn, token, inc, lwaits in ops:
                        for sem, val in waits:
                            e.wait_ge(sem, val)
                        if fn is None:
                            continue
                        inst = fn(e)
                        for sem, val in lwaits:
                            inst._wait_ge(sem, val)
                        if token is not None:
                            inst.then_inc(token[0], inc)
                return body

            block.tensor(mk("pe"))
            block.scalar(mk("act"))
            block.vector(mk("dve"))
            block.gpsimd(mk("pool"))
            block.sync(mk("sp"))

from contextlib import ExitStack

NLAT = 4096
NCTX = 256
NTOK = NLAT + NCTX
D = 2048
EPS = 1e-6
ALPHA_C = 4 ** 0.25
NTB = 9


def tb_info(tb):
    return (tb * 512, 512 if tb < 8 else 256)


def _barrier(P):
    deps = {}
    for q, slots in P.dma_slots.items():
        for s, n in slots:
            if n:
                deps[s] = 16 * n
    for n, e in P.engs.items():
        if e.count:
            idx = (e.count - 1) // P.EPOCH
            deps[e.sems[idx]] = (e.count - 1) % P.EPOCH + 1
    for n, e in P.engs.items():
        waits = []
        for sem, val in deps.items():
            if e.waited.get(id(sem), 0) < val:
                e.waited[id(sem)] = val
                waits.append((sem, val))
        e.ops.append((waits, None, None, 0, []))


def _flush(P):
    _barrier(P)
    P.emit()
    for e in P.engs.values():
        e.ops = []


class Stage:
    uid = 0
    sid = 0
    def __init__(self, nc, P):
        self.nc = nc
        self.P = P
        self.es = ExitStack()
        self.n = 0

    def sb(self, shape, dt, name=None):
        Stage.uid += 1
        t = self.es.enter_context(self.nc.sbuf_tensor(name or f"t{Stage.uid}", shape, dt))
        return t, Buf()

    def ps(self, shape, dt=F32, name=None):
        Stage.uid += 1
        t = self.es.enter_context(self.nc.psum_tensor(name or f"p{Stage.uid}", shape, dt))
        return t, Buf()

    def __enter__(self):
        return self

    def __exit__(self, *a):
        Stage.sid += 1
        with self.nc.named_scope(f"stage{Stage.sid}"):
            _flush(self.P)
        self.es.close()
        return False


def stage_modvec(nc, P, T, layer):
    with Stage(nc, P) as S:
        for _ in body_modvec(S, nc, P, T, layer, "sp"):
            pass


def body_modvec(S, nc, P, T, layer, dmaq):
    if True:
        cs, b_cs = S.sb([128, 16, 2], F32)
        s, b_s = S.sb([128, 16, 2], F32)
        wb = [S.sb([128, 3072], F32) for _ in range(2)]
        ps, b_ps = S.ps([2, 3072], F32)
        row, b_row = S.sb([2, 6144], F32)
        bm, b_bm = S.sb([2, 6144], F32)
        pst, b_pst = S.ps([128, 96], F32)
        mod, b_mod = T["mod_fm"][layer]
        ident, b_id = T["ident"]
        P.dma(dmaq, cs[:], T["cs"], writes=[b_cs])
        P.dma(dmaq, bm[:], T["b_mod"][layer:layer + 1, :].partition_broadcast(2), writes=[b_bm])
        P.op("act", lambda e: e.activation(out=s[:], in_=cs[:], func=AF.Silu), reads=[b_cs], writes=[b_s])
        i = 0
        for half in range(2):
            for k in range(16):
                w, b_w = wb[i % 2]
                i += 1
                P.dma(dmaq, w[:], T["w_mod"][layer, k * 128:(k + 1) * 128, half * 3072:(half + 1) * 3072], writes=[b_w])
                for j in range(6):
                    last = (j == 5)
                    P.op("pe", lambda e, w=w, j=j, k=k: e.matmul(ps[:, j * 512:(j + 1) * 512], lhsT=s[:, k, :], rhs=w[:, j * 512:(j + 1) * 512], start=(k == 0), stop=(k == 15)),
                         reads=[b_s, b_w], writes=[b_ps], signal=last, partial=(k > 0))
            if half == 1:
                yield
            P.op("dve", lambda e, half=half: e.tensor_tensor(out=row[:, half * 3072:(half + 1) * 3072], in0=ps[:], in1=bm[:, half * 3072:(half + 1) * 3072], op=ALU.add),
                 reads=[b_ps, b_bm], writes=[b_row], partial=(half > 0))
        P.dma("sp", T["modrow"][layer], row[:], reads=[b_row], writes=[T["b_modrow"][layer]])
        for j in range(48):
            P.op("pe", lambda e, j=j: e.transpose(pst[:, j * 2:(j + 1) * 2], row[0:2, j * 128:(j + 1) * 128], ident[0:2, 0:2]),
                 reads=[b_row, b_id], writes=[b_pst], signal=(j == 47), partial=(j > 0))
        P.op("dve", lambda e: e.tensor_copy(out=mod[:], in_=pst[:]), reads=[b_pst], writes=[b_mod])
        P.op("dve", lambda e: e.tensor_scalar_add(out=mod[:, 32:64], in0=mod[:, 32:64], scalar1=1.0), reads=[b_mod], writes=[b_mod])


def stage_xinT(nc, P, T, layer):
    with Stage(nc, P) as S:
        xt = [[S.sb([128, 2048], F32) for _ in range(4)] for _ in range(3)]
        xo = [S.sb([128, 16, 512], BF16) for _ in range(3)]
        pp = [S.ps([128, 512], F32) for _ in range(4)]
        mod, b_mod = T["mod_fm"][layer]
        ident, b_id = T["ident"]
        k = 0
        for tb in range(NTB):
            t0, n = tb_info(tb)
            nt = n // 128
            r = 0 if tb < 8 else 1
            st = tb % 3
            for tt in range(nt):
                tile, b_t = xt[st][tt]
                row0 = t0 + tt * 128
                if layer == 0:
                    src = T["x"][row0:row0 + 128, :] if tb < 8 else T["ctx"][tt * 128:(tt + 1) * 128, :]
                    P.dma("sp", tile[:], src, writes=[b_t])
                else:
                    P.dma("sp", tile[:], T["xres"][row0:row0 + 128, :], reads=[T["b_xres"]], writes=[b_t])
            o, b_o = xo[st]
            for c in range(16):
                p_, b_p = pp[k % 4]
                k += 1
                for tt in range(nt):
                    tile, b_t = xt[st][tt]
                    P.op("pe", lambda e, p_=p_, tile=tile, tt=tt, c=c: e.transpose(p_[:, tt * 128:(tt + 1) * 128], tile[:, c * 128:(c + 1) * 128], ident[:]),
                         reads=[b_t, b_id], writes=[b_p], signal=(tt == nt - 1), partial=(tt > 0))
                sc_ap = mod[:, (16 + c) * 2 + r:(16 + c) * 2 + r + 1]
                bi_ap = mod[:, c * 2 + r:c * 2 + r + 1]
                if c % 2 == 0:
                    P.op("act", lambda e, o=o, p_=p_, c=c, n=n, sc_ap=sc_ap, bi_ap=bi_ap: e.activation(out=o[:, c, 0:n], in_=p_[:, 0:n], func=AF.Identity, scale=sc_ap, bias=bi_ap),
                         reads=[b_p, b_mod], writes=[b_o], partial=(c > 0))
                else:
                    P.op("dve", lambda e, o=o, p_=p_, c=c, n=n, sc_ap=sc_ap, bi_ap=bi_ap: e.tensor_scalar(out=o[:, c, 0:n], in0=p_[:, 0:n], scalar1=sc_ap, scalar2=bi_ap, op0=ALU.mult, op1=ALU.add),
                         reads=[b_p, b_mod], writes=[b_o], partial=(c > 0))
            P.dma("sp", T["xinT"][tb][:, :, 0:n], o[:, :, 0:n], reads=[b_o], writes=[T["b_xinT"]], partial=True)


def rope_tile(P, S, src, b_src, n, t0, tabs, perm, pm, out, b_out, wk):
    (Ct, b_C), (St, b_S) = tabs
    pmt, b_pm = pm
    (t1, b_t1), (t2, b_t2) = wk
    permt, b_perm = perm
    P.op("pe", lambda e: e.matmul(pmt[:, 0:n], lhsT=permt[:], rhs=src[:, 0:n], start=True, stop=True), reads=[b_src, b_perm], writes=[b_pm])
    P.op("pool", lambda e: e.tensor_tensor(out=t1[:, 0:n], in0=src[:, 0:n], in1=Ct[:, t0:t0 + n], op=ALU.mult), reads=[b_src, b_C], writes=[b_t1])
    P.op("dve", lambda e: e.tensor_tensor(out=t2[:, 0:n], in0=pmt[:, 0:n], in1=St[:, t0:t0 + n], op=ALU.mult), reads=[b_pm, b_S], writes=[b_t2])
    P.op("pool", lambda e: e.tensor_tensor(out=out[:, 0:n], in0=t1[:, 0:n], in1=t2[:, 0:n], op=ALU.add), reads=[b_t1, b_t2], writes=[b_out])


def stage_proj(nc, P, T, layer):
    if layer == 0:
        w_in = T["w_in0"]
        groups = [
            (0, 1024, [("store", j, ("featA", j)) for j in range(8)]),
            (1024, 1024, [("store", j, ("featA", 8 + j)) for j in range(8)]),
            (2048, 1024, [("store", j, ("featA", 16 + j)) for j in range(8)]),
            (3072, 1152, [("store32", j, ("latT", j)) for j in range(9)]),
            (4224, 1024, [("silu", j, ("gateT", j)) for j in range(8)]),
            (5248, 1024, [("silu", j, ("gateT", 8 + j)) for j in range(8)]),
        ]
    else:
        w_in = T["w_in1"]
        groups = [
            (0, 1024, [("qk", j, ("featA", j), 0) for j in range(8)]),
            (1024, 1024, [("qk", j, ("featA", 8 + j), 0) for j in range(8)]),
            (2048, 1024, [("qk", j, ("featA", 16 + j), 1) for j in range(4)] + [("v", 4, None)]),
            (3072, 1024, [("silu", j, ("gateT", j)) for j in range(8)]),
            (4096, 1024, [("silu", j, ("gateT", 8 + j)) for j in range(8)]),
        ]
    wv = w_in.rearrange("(c p) n -> p c n", p=128)
    with Stage(nc, P) as S:
        wt = [S.sb([128, 16, 1152], BF16) for _ in range(2)]
        xb = [S.sb([128, 16, 512], BF16) for _ in range(2)]
        NQ = 3
        mm = [S.ps([128, 512], F32) for _ in range(4)]
        ob = [S.sb([128, 512], BF16) for _ in range(6)]
        ob32 = [S.sb([128, 512], F32) for _ in range(2)]
        if layer == 1:
            ssp = [S.ps([128, 512], F32) for _ in range(2)]
            pmp = [S.ps([128, 512], F32) for _ in range(2)]
            sq = [S.sb([128, 512], BF16) for _ in range(NQ)]
            rstd = [S.sb([128, 512], F32) for _ in range(NQ)]
            qn = [S.sb([128, 512], F32) for _ in range(NQ)]
            wk = [(S.sb([128, 512], F32), S.sb([128, 512], F32)) for _ in range(NQ)]
            Ct = S.sb([128, NTOK], F32)
            St = S.sb([128, NTOK], F32)
            P.dma("sp", Ct[0][:], T["rope1"][0], writes=[Ct[1]])
            P.dma("sp", St[0][:], T["rope1"][1], writes=[St[1]])
            qkn, b_qkn = S.sb([128, 2], F32)
            P.dma("sp", qkn[:], T["qk_norm1"], writes=[b_qkn])
            vo = [S.sb([128, 512], BF16) for _ in range(2)]
        onesm, b_onesm = T["ones128"]
        cnt = dict(mm=0, ob=0, ob32=0, x=0, q=0, vo=0)

        def load_w(gi):
            col0, ncols, _ = groups[gi]
            w, b_w = wt[gi % 2]
            for c4 in range(4):
                P.dma("pool", w[:, c4 * 4:(c4 + 1) * 4, 0:ncols], wv[:, c4 * 4:(c4 + 1) * 4, col0:col0 + ncols], writes=[b_w], partial=(c4 > 0))

        seq = [(gi, tb) for gi in range(len(groups)) for tb in range(NTB)]

        def load_x(i):
            gi, tb = seq[i]
            t0, n = tb_info(tb)
            x_, b_x = xb[i % 2]
            P.dma("sp", x_[:, :, 0:n], T["xinT"][tb][:, :, 0:n], reads=[T["b_xinT"]], writes=[b_x])

        def epilogue(item, p_, b_p, n, t0):
            kind = item[0]
            dname, didx = item[2]
            dst = T[dname][didx][:, t0:t0 + n]
            b_dst = T["b_" + dname]
            if kind == "store32":
                o_, b_o = ob32[cnt["ob32"] % 2]
                cnt["ob32"] += 1
                P.op("dve", lambda e: e.tensor_copy(out=o_[:, 0:n], in_=p_[:, 0:n]), reads=[b_p], writes=[b_o])
                P.dma("sp", dst, o_[:, 0:n], reads=[b_o], writes=[b_dst], partial=True)
                return
            o_, b_o = ob[cnt["ob"] % len(ob)]
            cnt["ob"] += 1
            if kind == "store":
                if cnt["ob"] % 2:
                    P.op("act", lambda e: e.copy(out=o_[:, 0:n], in_=p_[:, 0:n]), reads=[b_p], writes=[b_o])
                else:
                    P.op("dve", lambda e: e.tensor_copy(out=o_[:, 0:n], in_=p_[:, 0:n]), reads=[b_p], writes=[b_o])
            elif kind == "silu":
                P.op("act", lambda e: e.activation(out=o_[:, 0:n], in_=p_[:, 0:n], func=AF.Silu), reads=[b_p], writes=[b_o])
            elif kind == "qk":
                which = item[3]
                q = cnt["q"] % NQ
                cnt["q"] += 1
                sq_, b_sq = sq[q]
                ss_, b_ss = ssp[q % 2]
                rs_, b_rs = rstd[q]
                qn_, b_qn = qn[q]
                pmt, b_pm = pmp[q % 2]
                (t1, b_t1), (t2, b_t2) = wk[q]
                (Cx, b_C), (Sx, b_S) = Ct, St
                permt, b_perm = T["perm1"]
                P.op("act", lambda e: e.activation(out=sq_[:, 0:n], in_=p_[:, 0:n], func=AF.Square), reads=[b_p], writes=[b_sq])
                yield
                P.op("pe", lambda e: e.matmul(ss_[:, 0:n], lhsT=onesm[:], rhs=sq_[:, 0:n], start=True, stop=True), reads=[b_sq, b_onesm], writes=[b_ss])
                P.op("act", lambda e: e.activation(out=rs_[:, 0:n], in_=ss_[:, 0:n], func=AF.Ln, bias=EPS), reads=[b_ss], writes=[b_rs])
                P.op("act", lambda e: e.activation(out=rs_[:, 0:n], in_=rs_[:, 0:n], func=AF.Exp, scale=-0.5), reads=[b_rs], writes=[b_rs])
                P.op("dve", lambda e: e.scalar_tensor_tensor(out=qn_[:, 0:n], in0=p_[:, 0:n], scalar=qkn[:, which:which + 1], in1=rs_[:, 0:n], op0=ALU.mult, op1=ALU.mult),
                     reads=[b_p, b_rs, b_qkn], writes=[b_qn])
                yield
                P.op("pe", lambda e: e.matmul(pmt[:, 0:n], lhsT=permt[:], rhs=qn_[:, 0:n], start=True, stop=True), reads=[b_qn, b_perm], writes=[b_pm])
                P.op("pool", lambda e: e.tensor_tensor(out=t1[:, 0:n], in0=qn_[:, 0:n], in1=Cx[:, t0:t0 + n], op=ALU.mult), reads=[b_qn, b_C], writes=[b_t1])
                P.op("dve", lambda e: e.tensor_tensor(out=t2[:, 0:n], in0=pmt[:, 0:n], in1=Sx[:, t0:t0 + n], op=ALU.mult), reads=[b_pm, b_S], writes=[b_t2])
                P.op("dve", lambda e: e.tensor_tensor(out=o_[:, 0:n], in0=t1[:, 0:n], in1=t2[:, 0:n], op=ALU.add), reads=[b_t1, b_t2], writes=[b_o])
            P.dma("sp", dst, o_[:, 0:n], reads=[b_o], writes=[b_dst], partial=True)

        pending = []

        def step_pending():
            for gen in list(pending):
                try:
                    next(gen)
                except StopIteration:
                    pending.remove(gen)

        load_w(0)
        load_x(0)
        for i, (gi, tb) in enumerate(seq):
            col0, ncols, items = groups[gi]
            if tb == 0 and gi + 1 < len(groups):
                load_w(gi + 1)
            if i + 1 < len(seq):
                load_x(i + 1)
            t0, n = tb_info(tb)
            w, b_w = wt[gi % 2]
            x_, b_x = xb[i % 2]
            for item in items:
                kind, j = item[0], item[1]
                if kind == "v":
                    for tt in range(n // 128):
                        p_, b_p = mm[cnt["mm"] % len(mm)]
                        cnt["mm"] += 1
                        for c in range(16):
                            P.op("pe", lambda e, p_=p_, x_=x_, w=w, c=c, tt=tt, j=j: e.matmul(p_[:, 0:512], lhsT=x_[:, c, tt * 128:(tt + 1) * 128], rhs=w[:, c, j * 128:j * 128 + 512], start=(c == 0), stop=(c == 15)),
                                 reads=[b_x, b_w], writes=[b_p], signal=(c == 15), partial=(c > 0))
                        step_pending()
                        v_, b_v = vo[cnt["vo"] % 2]
                        cnt["vo"] += 1
                        P.op("dve", lambda e, v_=v_, p_=p_: e.tensor_copy(out=v_[:], in_=p_[:]), reads=[b_p], writes=[b_v])
                        r0 = t0 + tt * 128
                        P.dma("sp", T["vtok"][r0:r0 + 128, 0:512], v_[:], reads=[b_v], writes=[T["b_vtok"]], partial=True)
                    continue
                p_, b_p = mm[cnt["mm"] % len(mm)]
                cnt["mm"] += 1
                for c in range(16):
                    P.op("pe", lambda e, p_=p_, x_=x_, w=w, c=c, j=j, n=n: e.matmul(p_[:, 0:n], lhsT=w[:, c, j * 128:(j + 1) * 128], rhs=x_[:, c, 0:n], start=(c == 0), stop=(c == 15)),
                         reads=[b_x, b_w], writes=[b_p], signal=(c == 15), partial=(c > 0))
                step_pending()
                gen = epilogue(item, p_, b_p, n, t0)
                pending.append(gen)
                try:
                    next(gen)
                except StopIteration:
                    pending.remove(gen)
        while pending:
            step_pending()


def stage_conv(nc, P, T, with_mod=None):
    with Stage(nc, P) as S:
        g = None
        if with_mod is not None:
            g = body_modvec(S, nc, P, T, with_mod, "act")
            next(g)
        body_conv(S, nc, P, T)
        if g is not None:
            for _ in g:
                pass


def body_conv(S, nc, P, T):
    if True:
        cw, b_cw = S.sb([128, 8, 3], F32)
        P.dma("sp", cw[:], T["conv_w"], writes=[b_cw])
        NB = 4
        cbt = [S.sb([128, 512], BF16) for _ in range(NB)]
        cct = [S.sb([128, 514], BF16) for _ in range(NB)]
        cut = [S.sb([128, 514], BF16) for _ in range(NB)]
        gtt = [S.sb([128, 512], BF16) for _ in range(NB)]
        vt = [S.sb([128, 514], F32) for _ in range(NB)]
        at = [S.sb([128, 512], F32) for _ in range(NB)]
        ot = [S.sb([128, 512], BF16) for _ in range(NB)]
        rd = [T["b_featA"], T["b_gateT"]]
        tiles = [(g, tb) for g in range(8) for tb in range(NTB)]

        def info(i):
            g, tb = tiles[i]
            t0, n = tb_info(tb)
            first = (t0 == 0 or t0 == NLAT)
            last = (t0 + n == NLAT or t0 + n == NTOK)
            return g, t0, n, first, last, i % NB

        def phA(i):
            g, t0, n, first, last, k = info(i)
            lo = 1 if first else 0
            hi = n + 1 if last else n + 2
            (cb_, b_cb), (cc_, b_cc), (cu_, b_cu), (gt_, b_gt), (v_, b_v) = cbt[k], cct[k], cut[k], gtt[k], vt[k]
            P.dma("sp", cb_[:, 0:n], T["featA"][g][:, t0:t0 + n], reads=rd, writes=[b_cb])
            P.dma("sp", cc_[:, lo:hi], T["featA"][8 + g][:, t0 - 1 + lo:t0 - 1 + hi], reads=rd, writes=[b_cc])
            P.dma("sp", cu_[:, lo:hi], T["featA"][16 + g][:, t0 - 1 + lo:t0 - 1 + hi], reads=rd, writes=[b_cu])
            P.dma("sp", gt_[:, 0:n], T["gateT"][g][:, t0:t0 + n], reads=rd, writes=[b_gt])
            P.op("pool", lambda e: e.tensor_tensor(out=v_[:, lo:hi], in0=cc_[:, lo:hi], in1=cu_[:, lo:hi], op=ALU.mult), reads=[b_cc, b_cu], writes=[b_v])
            if first:
                P.op("pool", lambda e: e.memset(v_[:, 0:1], 0.0), reads=[], writes=[b_v])
            if last:
                P.op("pool", lambda e: e.memset(v_[:, n + 1:n + 2], 0.0), reads=[], writes=[b_v])

        def phB(i):
            g, t0, n, first, last, k = info(i)
            (v_, b_v), (a_, b_a) = vt[k], at[k]
            P.op("dve", lambda e: e.tensor_scalar_mul(out=a_[:, 0:n], in0=v_[:, 0:n], scalar1=cw[:, g, 0:1]), reads=[b_v, b_cw], writes=[b_a])
            P.op("dve", lambda e: e.scalar_tensor_tensor(out=a_[:, 0:n], in0=v_[:, 1:n + 1], scalar=cw[:, g, 1:2], in1=a_[:, 0:n], op0=ALU.mult, op1=ALU.add), reads=[b_v, b_cw, b_a], writes=[b_a])
            P.op("dve", lambda e: e.scalar_tensor_tensor(out=a_[:, 0:n], in0=v_[:, 2:n + 2], scalar=cw[:, g, 2:3], in1=a_[:, 0:n], op0=ALU.mult, op1=ALU.add), reads=[b_v, b_cw, b_a], writes=[b_a])

        def phC(i):
            g, t0, n, first, last, k = info(i)
            (cb_, b_cb), (gt_, b_gt), (a_, b_a), (o_, b_o) = cbt[k], gtt[k], at[k], ot[k]
            P.op("pool", lambda e: e.tensor_tensor(out=a_[:, 0:n], in0=a_[:, 0:n], in1=cb_[:, 0:n], op=ALU.mult), reads=[b_a, b_cb], writes=[b_a])
            P.op("pool", lambda e: e.tensor_tensor(out=o_[:, 0:n], in0=a_[:, 0:n], in1=gt_[:, 0:n], op=ALU.mult), reads=[b_a, b_gt], writes=[b_o])
            P.dma("sp", T["mixT"][g][:, t0:t0 + n], o_[:, 0:n], reads=[b_o], writes=[T["b_mixT"]], partial=True)

        NT = len(tiles)
        phA(0)
        phA(1)
        phB(0)
        for i in range(NT):
            if i + 2 < NT:
                phA(i + 2)
            if i + 1 < NT:
                phB(i + 1)
            phC(i)


def stage_mla(nc, P, T):
    with Stage(nc, P) as S:
        wqb, b_wqb = S.sb([128, 4, 1536], BF16)
        wkvb, b_wkvb = S.sb([128, 4, 2048], BF16)
        P.dma("pool", wqb[:], T["w_qb"].rearrange("(c p) n -> p c n", p=128), writes=[b_wqb])
        P.dma("pool", wkvb[:], T["w_kvb"].rearrange("(c p) n -> p c n", p=128), writes=[b_wkvb])
        nrm, b_nrm = S.sb([128, 2, 4], F32)
        P.dma("sp", nrm[:], T["norm0"], writes=[b_nrm])
        Ct = S.sb([128, NTOK], F32)
        St = S.sb([128, NTOK], F32)
        P.dma("sp", Ct[0][:], T["rope0"][0], writes=[Ct[1]])
        P.dma("sp", St[0][:], T["rope0"][1], writes=[St[1]])
        lat = [S.sb([128, 9, 512], F32) for _ in range(2)]
        sq = [S.sb([128, 512], BF16) for _ in range(2)]
        ssp = [S.ps([128, 512], F32) for _ in range(2)]
        rstd = [S.sb([128, 512], F32) for _ in range(2)]
        lnb = [S.sb([128, 4, 512], BF16) for _ in range(2)]
        mm = [S.ps([128, 512], F32) for _ in range(3)]
        pmp = [S.ps([128, 512], F32) for _ in range(2)]
        ob = [S.sb([128, 512], BF16) for _ in range(4)]
        o32 = [S.sb([128, 512], F32) for _ in range(2)]
        wk = [(S.sb([128, 512], F32), S.sb([128, 512], F32)) for _ in range(2)]
        ones512, b_ones = T["ones512"]
        cnt = dict(mm=0, ob=0, o32=0, q=0, ln=0)
        rdl = [T["b_latT"]]
        latv = T["latT_all"]

        def nxt(key, lst):
            k = cnt[key] % len(lst)
            cnt[key] += 1
            return lst[k]

        def norm_phase(tb, which):
            t0, n = tb_info(tb)
            l_, b_l = lat[tb % 2]
            if which == 0:
                for j in range(9):
                    P.dma("sp", l_[:, j, 0:n], T["latT"][j][:, t0:t0 + n], reads=rdl, writes=[b_l], partial=(j > 0))
            ss_, b_ss = nxt("q", ssp)
            for c in range(4):
                sq_, b_sq = sq[c % 2]
                P.op("act", lambda e, sq_=sq_, c=c: e.activation(out=sq_[:, 0:n], in_=l_[:, which * 4 + c, 0:n], func=AF.Square), reads=[b_l], writes=[b_sq])
                P.op("pe", lambda e, sq_=sq_, c=c: e.matmul(ss_[:, 0:n], lhsT=ones512[:], rhs=sq_[:, 0:n], start=(c == 0), stop=(c == 3)), reads=[b_sq, b_ones], writes=[b_ss], partial=(c > 0))
            rs_, b_rs = rstd[which]
            P.op("act", lambda e: e.activation(out=rs_[:, 0:n], in_=ss_[:, 0:n], func=AF.Ln, bias=EPS), reads=[b_ss], writes=[b_rs])
            P.op("act", lambda e: e.activation(out=rs_[:, 0:n], in_=rs_[:, 0:n], func=AF.Exp, scale=-0.5), reads=[b_rs], writes=[b_rs])
            ln_, b_ln = lnb[which]
            for c in range(4):
                P.op("dve", lambda e, c=c: e.scalar_tensor_tensor(out=ln_[:, c, 0:n], in0=l_[:, which * 4 + c, 0:n], scalar=nrm[:, which, c:c + 1], in1=rs_[:, 0:n], op0=ALU.mult, op1=ALU.mult),
                     reads=[b_l, b_rs, b_nrm], writes=[b_ln], partial=(c > 0))

        def mm_phase(tb, which):
            t0, n = tb_info(tb)
            l_, b_l = lat[tb % 2]
            ln_, b_ln = lnb[which]
            wmat, b_wm = (wqb, b_wqb) if which == 0 else (wkvb, b_wkvb)
            nblk = 12 if which == 0 else 8
            for jb in range(nblk):
                p_, b_p = nxt("mm", mm)
                for c in range(4):
                    P.op("pe", lambda e, p_=p_, c=c, jb=jb: e.matmul(p_[:, 0:n], lhsT=wmat[:, c, jb * 128:(jb + 1) * 128], rhs=ln_[:, c, 0:n], start=(c == 0), stop=(c == 3)),
                         reads=[b_ln, b_wm], writes=[b_p], signal=(c == 3), partial=(c > 0))
                o_, b_o = nxt("ob", ob)
                if which == 0 and jb >= 8:
                    q32, b_q32 = nxt("o32", o32)
                    P.op("act", lambda e, q32=q32, p_=p_: e.copy(out=q32[:, 0:n], in_=p_[:, 0:n]), reads=[b_p], writes=[b_q32])
                    k2 = cnt["o32"] % 2
                    rope_tile(P, S, q32, b_q32, n, t0, (Ct, St), T["perm0"], pmp[k2], o_, b_o, wk[k2])
                    dst = T["qkT"][jb][:, t0:t0 + n]
                else:
                    P.op("act", lambda e, o_=o_, p_=p_: e.copy(out=o_[:, 0:n], in_=p_[:, 0:n]), reads=[b_p], writes=[b_o])
                    dst = T["qkT"][jb if which == 0 else 12 + jb][:, t0:t0 + n]
                P.dma("sp", dst, o_[:, 0:n], reads=[b_o], writes=[T["b_qkT"]], partial=True)
            if which == 1:
                for tt in range(n // 128):
                    for hf in range(2):
                        p_, b_p = nxt("mm", mm)
                        for c in range(4):
                            P.op("pe", lambda e, p_=p_, c=c, tt=tt, hf=hf: e.matmul(p_[:, 0:512], lhsT=ln_[:, c, tt * 128:(tt + 1) * 128], rhs=wkvb[:, c, 1024 + hf * 512:1024 + (hf + 1) * 512], start=(c == 0), stop=(c == 3)),
                                 reads=[b_ln, b_wkvb], writes=[b_p], signal=(c == 3), partial=(c > 0))
                        o_, b_o = nxt("ob", ob)
                        P.op("dve", lambda e, o_=o_, p_=p_: e.tensor_copy(out=o_[:], in_=p_[:]), reads=[b_p], writes=[b_o])
                        r0 = t0 + tt * 128
                        P.dma("sp", T["vtok"][r0:r0 + 128, hf * 512:(hf + 1) * 512], o_[:], reads=[b_o], writes=[T["b_vtok"]], partial=True)
                kr32, b_kr = nxt("o32", o32)
                P.op("act", lambda e: e.copy(out=kr32[:, 0:n], in_=l_[:, 8, 0:n]), reads=[b_l], writes=[b_kr])
                k2 = cnt["o32"] % 2
                o_, b_o = nxt("ob", ob)
                rope_tile(P, S, kr32, b_kr, n, t0, (Ct, St), T["perm0"], pmp[k2], o_, b_o, wk[k2])
                P.dma("sp", T["qkT"][20][:, t0:t0 + n], o_[:, 0:n], reads=[b_o], writes=[T["b_qkT"]], partial=True)

        steps = [(tb, which) for tb in range(NTB) for which in range(2)]
        norm_phase(*steps[0])
        for si, st_ in enumerate(steps):
            if si + 1 < len(steps):
                norm_phase(*steps[si + 1])
            mm_phase(*st_)


def stage_attn(nc, P, T, layer):
    H = 8 if layer == 0 else 16
    G = 1 if layer == 0 else 4
    scale = (192 ** -0.5) if layer == 0 else (128 ** -0.5)
    NKT = NTOK // 128
    POOL_SLOTS = (2, 5, 8) if layer == 0 else (1, 3, 5, 7)
    with Stage(nc, P) as S:
        KT = [S.sb([128, NTOK], BF16) for _ in range(2)]
        V = [S.sb([128, NKT, 128], BF16) for _ in range(2)]
        QT = [S.sb([128, NTOK], BF16) for _ in range(2)]
        if layer == 0:
            KR, b_KR = S.sb([128, NTOK], BF16)
            QR = [S.sb([128, NTOK], BF16) for _ in range(2)]
            P.dma("sp", KR[:], T["qkT"][20], reads=[T["b_qkT"]], writes=[b_KR])
        gt = [S.sb([128, 512], BF16) for _ in range(2)]
        PT = [S.sb([128, 1024], BF16) for _ in range(8)]
        accd = [S.sb([128, 1024], F32) for _ in range(2)]
        accp = [S.sb([128, 1024], F32) for _ in range(2)]
        rinv = [S.sb([128, 512], F32) for _ in range(2)]
        rg = [S.sb([128, 512], F32) for _ in range(2)]
        mo = [S.sb([128, 512], BF16) for _ in range(2)]
        sps = [S.ps([128, 1024], F32) for _ in range(2)]
        Ops = [S.ps([128, 512], F32) for _ in range(2)]
        RS = S.ps([128, 512], F32)
        onesf, b_onesf = T["onesf"]
        cnt = dict(s=0, pt=0, qb=0)
        srcq = T["b_qkT"] if layer == 0 else T["b_featA"]
        vview = T["vtok"].rearrange("(t p) d -> p t d", p=128)
        tails = []

        def step_tails(drain_parity=None):
            for ent in list(tails):
                tq, gen = ent
                if drain_parity is not None:
                    if tq % 2 == drain_parity:
                        for _ in gen:
                            pass
                        tails.remove(ent)
                    continue
                try:
                    next(gen)
                except StopIteration:
                    tails.remove(ent)

        def tail(qi, q0, nq, two, usedpool, o_, b_o, ad_, b_ad, ap_, b_ap, g_, b_g, mblk):
            yield
            if usedpool:
                P.op("dve", lambda e: e.tensor_tensor(out=ad_[:, 0:2 * nq], in0=ad_[:, 0:2 * nq], in1=ap_[:, 0:2 * nq], op=ALU.add), reads=[b_ad, b_ap], writes=[b_ad])
            yield
            rs_, b_rs = RS
            nh = 2 if two else 1
            for hf in range(nh):
                P.op("pe", lambda e, hf=hf: e.matmul(rs_[:, 0:nq], lhsT=onesf[:], rhs=ad_[:, hf * nq:(hf + 1) * nq], start=(hf == 0), stop=(hf == nh - 1)), reads=[b_ad, b_onesf], writes=[b_rs], signal=(hf == nh - 1), partial=(hf > 0))
            yield
            ri_, b_ri = rinv[qi % 2]
            rg_, b_rg = rg[qi % 2]
            P.op("act", lambda e: e.activation(out=ri_[:, 0:nq], in_=rs_[:, 0:nq], func=AF.Ln), reads=[b_rs], writes=[b_ri])
            P.op("act", lambda e: e.activation(out=ri_[:, 0:nq], in_=ri_[:, 0:nq], func=AF.Exp, scale=-1.0), reads=[b_ri], writes=[b_ri])
            yield
            P.op("pool", lambda e: e.tensor_tensor(out=rg_[:, 0:nq], in0=ri_[:, 0:nq], in1=g_[:, 0:nq], op=ALU.mult), reads=[b_ri, b_g], writes=[b_rg])
            yield
            m_, b_m = mo[qi % 2]
            P.op("dve", lambda e: e.tensor_tensor(out=m_[:, 0:nq], in0=o_[:, 0:nq], in1=rg_[:, 0:nq], op=ALU.mult), reads=[b_o, b_rg], writes=[b_m])
            P.dma("sp", T["mixT"][mblk][:, q0:q0 + nq], m_[:, 0:nq], reads=[b_m], writes=[T["b_mixT"]], partial=True)

        def load_kv(hk):
            kt_, b_kt = KT[hk % 2]
            v_, b_v = V[hk % 2]
            if layer == 0:
                P.dma("sp", kt_[:], T["qkT"][12 + hk], reads=[srcq], writes=[b_kt])
            else:
                P.dma("sp", kt_[:], T["featA"][16 + hk], reads=[srcq], writes=[b_kt])
            voff = hk * 128
            P.dma("sp", v_[:], vview[:, :, voff:voff + 128], reads=[T["b_vtok"]], writes=[b_v])

        def load_q(h):
            qt_, b_qt = QT[h % 2]
            if layer == 0:
                P.dma("sp", qt_[:], T["qkT"][h], reads=[srcq], writes=[b_qt])
                qr_, b_qr = QR[h % 2]
                P.dma("sp", qr_[:], T["qkT"][8 + h // 2], reads=[srcq], writes=[b_qr])
            else:
                P.dma("sp", qt_[:, 0:NLAT], T["featA"][h][:, 0:NLAT], reads=[srcq], writes=[b_qt])

        load_kv(0)
        load_q(0)
        for hk in range(H // G):
            kt_, b_kt = KT[hk % 2]
            v_, b_v = V[hk % 2]
            for g in range(G):
                h = hk * G + g
                if h + 1 < H:
                    if (h + 1) % G == 0:
                        load_kv((h + 1) // G)
                    load_q(h + 1)
                qt_, b_qt = QT[h % 2]
                if layer == 0:
                    qr_, b_qr = QR[h % 2]
                    hb = (h % 2) * 64
                    gblk = 8 + h
                    qblocks = [(qb * 512, 512, list(range(NKT))) for qb in range(8)] + [(NLAT, 256, [32, 33])]
                    KRt, qrt, b_extra, hbv = KR, qr_, [b_KR, b_qr], hb
                else:
                    gblk = h
                    qblocks = [(qb * 512, 512, list(range(NKT))) for qb in range(8)]
                    KRt, qrt, b_extra, hbv = None, None, [], 0
                for (q0, nq, kts) in qblocks:
                    qi = cnt["qb"]
                    cnt["qb"] += 1
                    step_tails(drain_parity=qi % 2)
                    g_, b_g = gt[qi % 2]
                    P.dma("sp", g_[:, 0:nq], T["gateT"][gblk][:, q0:q0 + nq], reads=[T["b_gateT"]], writes=[b_g])
                    o_, b_o = Ops[qi % 2]
                    ad_, b_ad = accd[qi % 2]
                    ap_, b_ap = accp[qi % 2]
                    pairs = [kts[i:i + 2] for i in range(0, len(kts), 2)]
                    npair = len(pairs)
                    sbase = cnt["s"]
                    used = {"dve": False, "pool": False}

                    def qk(i, pairs=pairs, sbase=sbase, kt_=kt_, qt_=qt_, q0=q0, nq=nq, KRt=KRt, qrt=qrt, hbv=hbv, b_kt=b_kt, b_qt=b_qt, b_extra=b_extra):
                        s_, b_s = sps[(sbase + i) % 2]
                        for hf, kt in enumerate(pairs[i]):
                            lastm = (hf == len(pairs[i]) - 1)
                            if layer == 0:
                                P.op("pe", lambda e, s_=s_, kt=kt, hf=hf: e.matmul(s_[:, hf * nq:(hf + 1) * nq], lhsT=kt_[:, kt * 128:(kt + 1) * 128], rhs=qt_[:, q0:q0 + nq], start=True, stop=False),
                                     reads=[b_kt], late_reads=[b_qt], late_writes=[b_s], signal=False, partial=(hf > 0))
                                P.op("pe", lambda e, s_=s_, kt=kt, hf=hf: e.matmul(s_[:, hf * nq:(hf + 1) * nq], lhsT=KRt[hbv:hbv + 64, kt * 128:(kt + 1) * 128], rhs=qrt[hbv:hbv + 64, q0:q0 + nq], start=False, stop=True),
                                     reads=[b_kt, b_qt] + b_extra, writes=[b_s], signal=lastm, partial=True)
                            else:
                                P.op("pe", lambda e, s_=s_, kt=kt, hf=hf: e.matmul(s_[:, hf * nq:(hf + 1) * nq], lhsT=kt_[:, kt * 128:(kt + 1) * 128], rhs=qt_[:, q0:q0 + nq], start=True, stop=True),
                                     reads=[b_kt], late_reads=[b_qt], late_writes=[b_s], signal=lastm, partial=(hf > 0))

                    qk(0)
                    for i in range(npair):
                        if i + 1 < npair:
                            qk(i + 1)
                        s_, b_s = sps[(sbase + i) % 2]
                        p_, b_p = PT[cnt["pt"] % 8]
                        cnt["pt"] += 1
                        w = len(pairs[i]) * nq
                        P.op("act", lambda e, p_=p_, s_=s_, w=w: e.activation(out=p_[:, 0:w], in_=s_[:, 0:w], func=AF.Exp, scale=scale), reads=[b_s], writes=[b_p])
                        for hf, kt in enumerate(pairs[i]):
                            first = (i == 0 and hf == 0)
                            lastm = (hf == len(pairs[i]) - 1)
                            P.op("pe", lambda e, o_=o_, p_=p_, hf=hf, kt=kt, first=first, lastall=(i == npair - 1 and lastm), v_=v_, nq=nq: e.matmul(o_[:, 0:nq], lhsT=v_[:, kt, :], rhs=p_[:, hf * nq:(hf + 1) * nq], start=first, stop=lastall),
                                 reads=[b_v], late_reads=[b_p], late_writes=[b_o], signal=lastm, partial=not first)
                        if npair == 1:
                            P.op("dve", lambda e, ad_=ad_, p_=p_, w=w: e.tensor_copy(out=ad_[:, 0:w], in_=p_[:, 0:w]), reads=[b_p], writes=[b_ad])
                        elif i == 0:
                            p0_, b_p0 = p_, b_p
                        elif i == 1:
                            P.op("dve", lambda e, ad_=ad_, p_=p_, p0_=p0_, w=w: e.tensor_tensor(out=ad_[:, 0:w], in0=p0_[:, 0:w], in1=p_[:, 0:w], op=ALU.add), reads=[b_p, b_p0], writes=[b_ad])
                        else:
                            P.op("dve", lambda e, ad_=ad_, p_=p_, w=w: e.tensor_tensor(out=ad_[:, 0:w], in0=ad_[:, 0:w], in1=p_[:, 0:w], op=ALU.add), reads=[b_p, b_ad], writes=[b_ad])
                        step_tails()
                    cnt["s"] += npair
                    tails.append((qi, tail(qi, q0, nq, True, False, o_, b_o, ad_, b_ad, ap_, b_ap, g_, b_g, (8 + h) if layer == 0 else h)))
        while tails:
            step_tails()


def load_wo(nc, P, T, layer, es):
    wo = es.enter_context(nc.sbuf_tensor(f"wo{layer}", [128, 16, 2048], BF16))
    b_wo = Buf()
    wsrc = (T["w_out0"] if layer == 0 else T["w_out1"]).rearrange("(c p) n -> p c n", p=128)
    for c4 in range(4):
        P.dma("pool", wo[:, c4 * 4:(c4 + 1) * 4, :], wsrc[:, c4 * 4:(c4 + 1) * 4, :], writes=[b_wo], partial=(c4 > 0))
    return wo, b_wo


def stage_out(nc, P, T, layer, wo_pre):
    with Stage(nc, P) as S:
        wo, b_wo = wo_pre
        gbc = [S.sb([128, 2048], F32) for _ in range(2 if layer == 0 else 1)]
        for r in range(len(gbc)):
            P.dma("sp", gbc[r][0][:], T["modrow"][layer][r:r + 1, 4096:6144].partition_broadcast(128), reads=[T["b_modrow"][layer]], writes=[gbc[r][1]])
        lg, b_lg = S.sb([128, 2048], F32)
        lb, b_lb = S.sb([128, 2048], F32)
        P.dma("sp", lg[:], T["ln_g"][layer:layer + 1, :].partition_broadcast(128), writes=[b_lg])
        P.dma("sp", lb[:], T["ln_b"][layer:layer + 1, :].partition_broadcast(128), writes=[b_lb])
        mt = [S.sb([128, 16, 512], BF16) for _ in range(2)]
        xt = [S.sb([128, 2048], F32) for _ in range(2)]
        za = [S.sb([128, 2048], F32) for _ in range(2)]
        zb = [S.sb([128, 2048], F32) for _ in range(2)]
        junk, b_junk = S.sb([128, 2048], BF16)
        st = [S.sb([128, 8], F32) for _ in range(2)]
        yps = [[S.ps([128, 512], F32) for _ in range(4)] for _ in range(2)]
        ntb = NTB if layer == 0 else 8
        ti = 0
        tails = []

        def step_tails(drain_parity=None):
            for ent in list(tails):
                tq, gen = ent
                if drain_parity is not None:
                    if tq % 2 == drain_parity:
                        for _ in gen:
                            pass
                        tails.remove(ent)
                    continue
                try:
                    next(gen)
                except StopIteration:
                    tails.remove(ent)

        def tail(a_, b_a, b_, b_b, s_, b_s, row0):
            yield
            P.op("dve", lambda e: e.tensor_scalar_mul(out=s_[:, 2:4], in0=s_[:, 0:2], scalar1=1.0 / D), reads=[b_s], writes=[b_s])
            P.op("dve", lambda e: e.scalar_tensor_tensor(out=s_[:, 4:5], in0=s_[:, 2:3], scalar=s_[:, 2:3], in1=s_[:, 3:4], op0=ALU.mult, op1=ALU.subtract), reads=[b_s], writes=[b_s])
            yield
            P.op("act", lambda e: e.activation(out=s_[:, 5:6], in_=s_[:, 4:5], func=AF.Sqrt, bias=EPS, scale=-1.0), reads=[b_s], writes=[b_s])
            yield
            P.op("dve", lambda e: e.reciprocal(out=s_[:, 6:7], in_=s_[:, 5:6]), reads=[b_s], writes=[b_s])
            P.op("dve", lambda e: e.scalar_tensor_tensor(out=s_[:, 7:8], in0=s_[:, 2:3], scalar=-1.0, in1=s_[:, 6:7], op0=ALU.mult, op1=ALU.mult), reads=[b_s], writes=[b_s])
            yield
            P.op("act", lambda e: e.activation(out=b_[:], in_=a_[:], func=AF.Identity, scale=s_[:, 6:7], bias=s_[:, 7:8]), reads=[b_a, b_s], writes=[b_b])
            yield
            P.op("dve", lambda e: e.tensor_tensor(out=b_[:], in0=b_[:], in1=lg[:], op=ALU.mult), reads=[b_b, b_lg], writes=[b_b])
            P.op("dve", lambda e: e.tensor_tensor(out=b_[:], in0=b_[:], in1=lb[:], op=ALU.add), reads=[b_b, b_lb], writes=[b_b])
            if layer == 0:
                P.dma("sp", T["xres"][row0:row0 + 128, :], b_[:], reads=[b_b], writes=[T["b_xres"]], partial=True)
            else:
                P.dma("sp", T["out"][row0:row0 + 128, :], b_[:], reads=[b_b])

        def load_mix(tb):
            t0, n = tb_info(tb)
            m_, b_m = mt[tb % 2]
            for c in range(16):
                P.dma("sp", m_[:, c, 0:n], T["mixT"][c][:, t0:t0 + n], reads=[T["b_mixT"]], writes=[b_m], partial=(c > 0))

        load_mix(0)
        for tb in range(ntb):
            t0, n = tb_info(tb)
            r = 0 if tb < 8 else 1
            m_, b_m = mt[tb % 2]
            if tb + 1 < ntb:
                load_mix(tb + 1)
            for tt in range(n // 128):
                k = ti % 2
                step_tails(drain_parity=k)
                ti += 1
                row0 = t0 + tt * 128
                x_, b_x = xt[k]
                if layer == 0:
                    src = T["x"][row0:row0 + 128, :] if tb < 8 else T["ctx"][tt * 128:(tt + 1) * 128, :]
                    P.dma("sp", x_[:], src, writes=[b_x])
                else:
                    P.dma("sp", x_[:], T["xres"][row0:row0 + 128, :], reads=[T["b_xres"]], writes=[b_x])
                a_, b_a = za[k]
                b_, b_b = zb[k]
                s_, b_s = st[k]
                g_, b_g = gbc[r]
                for nb in range(4):
                    y_, b_y = yps[k][nb]
                    for c in range(16):
                        P.op("pe", lambda e, y_=y_, m_=m_, c=c, tt=tt, nb=nb: e.matmul(y_[:], lhsT=m_[:, c, tt * 128:(tt + 1) * 128], rhs=wo[:, c, nb * 512:(nb + 1) * 512], start=(c == 0), stop=(c == 15)),
                             reads=[b_m, b_wo], writes=[b_y], signal=(c == 15), partial=(c > 0))
                    P.op("dve", lambda e, a_=a_, y_=y_, g_=g_, nb=nb: e.tensor_tensor(out=a_[:, nb * 512:(nb + 1) * 512], in0=y_[:], in1=g_[:, nb * 512:(nb + 1) * 512], op=ALU.mult),
                         reads=[b_y, b_g], writes=[b_a], partial=(nb > 0))
                    step_tails()
                P.op("dve", lambda e, a_=a_, x_=x_, s_=s_: e.scalar_tensor_tensor(out=a_[:], in0=x_[:], scalar=ALPHA_C, in1=a_[:], op0=ALU.mult, op1=ALU.add, accum_out=s_[:, 0:1]),
                     reads=[b_x, b_a], writes=[b_a, b_s])
                step_tails()
                P.op("act", lambda e, a_=a_, s_=s_: e.activation(out=junk[:], in_=a_[:], func=AF.Square, accum_out=s_[:, 1:2]), reads=[b_a, b_s], writes=[b_junk, b_s])
                step_tails()
                tails.append((k, tail(a_, b_a, b_, b_b, s_, b_s, row0)))
        while tails:
            step_tails()


STAGES = ["mod0", "mod1", "xin0", "proj0", "conv0", "mla0", "ao0", "xin1", "proj1", "ao1"]


def build_nc(debug=False, stop_after=None):
    Stage.sid = 0
    nc = bass.Bass("TRN2", target_bir_lowering=False)
    T = {}

    def inp(name, shape):
        T[name] = nc.dram_tensor(name, shape, F32, kind="ExternalInput").ap()

    inp("x", [NLAT, D]); inp("ctx", [NCTX, D]); inp("cs", [128, 16, 2])
    inp("w_mod", [2, D, 6144]); inp("b_mod", [2, 6144]); inp("ln_g", [2, D]); inp("ln_b", [2, D])
    inp("w_in0", [D, 6272]); inp("conv_w", [128, 8, 3]); inp("norm0", [128, 2, 4])
    inp("w_qb", [512, 1536]); inp("w_kvb", [512, 2048]); inp("w_out0", [D, D])
    inp("w_in1", [D, 5120]); inp("qk_norm1", [128, 2]); inp("w_out1", [D, D])
    inp("c_ident", [128, 128]); inp("c_perm0", [128, 128]); inp("c_perm1", [128, 128])
    inp("rope0", [2, 128, NTOK]); inp("rope1", [2, 128, NTOK])
    T["out"] = nc.dram_tensor("out", [NLAT, D], F32, kind="ExternalOutput").ap()
    skind = "ExternalOutput" if debug else "Internal"

    def scr(name, shape, dt):
        T[name + "_all"] = nc.dram_tensor(name, shape, dt, kind=skind).ap()
        T[name] = T[name + "_all"]
        T["b_" + name] = Buf(name)

    scr("modrow", [2, 2, 6144], F32)
    T["b_modrow"] = [Buf(), Buf()]
    scr("xinT", [NTB, 128, 16, 512], BF16)
    scr("featA", [24, 128, NTOK], BF16)
    scr("gateT", [16, 128, NTOK], BF16)
    scr("latT", [9, 128, NTOK], F32)
    scr("qkT", [21, 128, NTOK], BF16)
    scr("vtok", [NTOK, 1024], BF16)
    scr("mixT", [16, 128, NTOK], BF16)
    scr("xres", [NTOK, D], F32)

    P = Prog(nc)
    with ExitStack() as es:
        def psb(name, shape, dt):
            return es.enter_context(nc.sbuf_tensor(name, shape, dt)), Buf(name)
        T["ident"] = psb("ident", [128, 128], F32)
        T["identb"] = psb("identb", [128, 128], BF16)
        T["perm0"] = psb("perm0", [128, 128], F32)
        T["perm1"] = psb("perm1", [128, 128], F32)
        T["ones128"] = psb("ones128", [128, 128], BF16)
        T["ones512"] = psb("ones512", [128, 128], BF16)
        T["onesf"] = psb("onesf", [128, 128], F32)
        T["mod_fm"] = [psb("modfm0", [128, 96], F32), psb("modfm1", [128, 96], F32)]
        P.dma("sp", T["ident"][0][:], T["c_ident"], writes=[T["ident"][1]])
        P.dma("pool", T["identb"][0][:], T["c_ident"], writes=[T["identb"][1]])
        P.dma("sp", T["perm0"][0][:], T["c_perm0"], writes=[T["perm0"][1]])
        P.dma("sp", T["perm1"][0][:], T["c_perm1"], writes=[T["perm1"][1]])
        P.op("pool", lambda e: e.memset(T["ones128"][0][:], 1.0 / 128), writes=[T["ones128"][1]])
        P.op("pool", lambda e: e.memset(T["ones512"][0][:], 1.0 / 512), writes=[T["ones512"][1]])
        P.op("pool", lambda e: e.memset(T["onesf"][0][:], 1.0), writes=[T["onesf"][1]])
        def attn_out(layer):
            with ExitStack() as es2:
                wo_pre = load_wo(nc, P, T, layer, es2)
                stage_attn(nc, P, T, layer)
                stage_out(nc, P, T, layer, wo_pre)

        fns = {
            "mod0": lambda: stage_modvec(nc, P, T, 0), "mod1": lambda: stage_modvec(nc, P, T, 1),
            "xin0": lambda: stage_xinT(nc, P, T, 0), "proj0": lambda: stage_proj(nc, P, T, 0),
            "conv0": lambda: stage_conv(nc, P, T), "mla0": lambda: stage_mla(nc, P, T),
            "ao0": lambda: attn_out(0),
            "xin1": lambda: stage_xinT(nc, P, T, 1), "proj1": lambda: stage_proj(nc, P, T, 1),
            "ao1": lambda: attn_out(1),
        }
        for sname in STAGES:
            fns[sname]()
            if stop_after == sname:
                break
        P.finish()
        _flush(P)
    return nc


def _rope_tables():
    f = np.float32
    pr = np.repeat(np.arange(64), 64).astype(f)
    pc = np.tile(np.arange(64), 64).astype(f)

    def tab(dper, nrep):
        half = dper // 2
        inv = (10000.0 ** (-np.arange(half, dtype=f) / f(half))).astype(f)
        C = np.ones((2 * dper, NTOK), f)
        Sg = np.zeros((2 * dper, NTOK), f)
        for blk, pos in enumerate((pr, pc)):
            ang = (pos[:, None] * inv[None, :]).astype(f)
            c = np.cos(ang).T.astype(f)
            s = np.sin(ang).T.astype(f)
            base = blk * dper
            C[base:base + half, :NLAT] = c
            C[base + half:base + dper, :NLAT] = c
            Sg[base:base + half, :NLAT] = -s
            Sg[base + half:base + dper, :NLAT] = s
        return np.stack([np.tile(C, (nrep, 1)), np.tile(Sg, (nrep, 1))]).astype(f)

    def perm(dper, nrep):
        half = dper // 2
        n = 2 * dper * nrep
        Pm = np.zeros((n, n), f)
        for d_ in range(n):
            i = d_ % dper
            partner = d_ + half if i < half else d_ - half
            Pm[partner, d_] = 1.0
        return Pm

    return tab(32, 2), tab(64, 1), perm(32, 2), perm(64, 1)


def make_in_maps(inputs):
    f = np.float32
    g = lambda k: np.asarray(inputs[k], dtype=f)
    x, c, ctx, c_ctx = g("x"), g("c"), g("ctx"), g("c_ctx")
    a_w_in = g("a_w_in")[0]
    w_in0 = np.ascontiguousarray(np.concatenate([a_w_in[:, :4160], a_w_in[:, 4096:4160], a_w_in[:, 4160:]], axis=1))
    conv_w = np.ascontiguousarray(g("a_conv_w")[0].reshape(3, 8, 128).transpose(2, 1, 0))
    norm0 = np.ascontiguousarray(np.stack([g("a_q_norm")[0].reshape(4, 128), g("a_kv_norm")[0].reshape(4, 128)]).transpose(2, 0, 1))
    wq = g("a_w_qb")[0].reshape(512, 8, 192)
    w_qb = np.ascontiguousarray(np.concatenate([wq[:, :, :128].reshape(512, 1024), wq[:, :, 128:].reshape(512, 512)], axis=1))
    wk = g("a_w_kvb")[0].reshape(512, 8, 256)
    w_kvb = np.ascontiguousarray(np.concatenate([wk[:, :, :128].reshape(512, 1024), wk[:, :, 128:].reshape(512, 1024)], axis=1))
    qk_norm1 = np.ascontiguousarray(np.stack([g("c_q_norm")[0], g("c_k_norm")[0]], axis=1))
    rope0, rope1, perm0, perm1 = _rope_tables()
    shared = {
        "w_mod": g("w_mod"), "b_mod": g("b_mod"), "ln_g": g("ln_g"), "ln_b": g("ln_b"),
        "w_in0": w_in0, "conv_w": conv_w, "norm0": norm0, "w_qb": w_qb, "w_kvb": w_kvb,
        "w_out0": np.ascontiguousarray(g("a_w_out")[0]), "w_in1": np.ascontiguousarray(g("c_w_in")[0]),
        "qk_norm1": qk_norm1, "w_out1": np.ascontiguousarray(g("c_w_out")[0]),
        "c_ident": np.eye(128, dtype=f), "c_perm0": perm0, "c_perm1": perm1, "rope0": rope0, "rope1": rope1,
    }
    maps = []
    for b in range(x.shape[0]):
        cs = np.ascontiguousarray(np.stack([c[b].reshape(16, 128).T, c_ctx.reshape(16, 128).T], axis=2))
        m = dict(shared)
        m.update({"x": np.ascontiguousarray(x[b]), "ctx": np.ascontiguousarray(ctx[b]), "cs": cs})
        maps.append(m)
    return maps


def kernel(**inputs):
    maps = make_in_maps(inputs)
    nc = build_nc()
    res = run_bass_kernel_spmd(nc, maps, core_ids=list(range(len(maps))))
    return np.stack([np.asarray(r["out"], dtype=np.float32) for r in res.results], axis=0)
```

```python
import numpy as np
import concourse.bass as bass
import concourse.mybir as mybir
from concourse.bass_utils import run_bass_kernel_spmd

F32 = mybir.dt.float32
BF16 = mybir.dt.bfloat16
ALU = mybir.AluOpType
AF = mybir.ActivationFunctionType
AX = mybir.AxisListType


class Buf:
    __slots__ = ("writers", "readers", "name")

    def __init__(self, name=""):
        self.writers = {}
        self.readers = {}
        self.name = name


class _Eng:
    def __init__(self, name):
        self.name = name
        self.ops = []
        self.sems = []
        self.count = 0
        self.waited = {}


class Prog:
    EPOCH = 30000
    SAME_ENG_SYNC = True

    def __init__(self, nc, n_dma_slots=16):
        self.nc = nc
        self.engs = {n: _Eng(n) for n in ("pe", "act", "dve", "pool", "sp")}
        self.dma_slots = {}
        self.dma_rr = {}
        self.n_dma_slots = n_dma_slots
        self.semkeys = {}

    def _merge(self, d, src):
        for k, v in src.items():
            if d.get(k, 0) < v:
                d[k] = v

    def _deps(self, reads, writes, partial):
        deps = {}
        for b in reads:
            self._merge(deps, b.writers)
        for b in writes:
            self._merge(deps, b.readers)
            if not partial:
                self._merge(deps, b.writers)
        return deps

    def _waits(self, eng, deps):
        waits = []
        for sem, val in deps.items():
            owner = self.semkeys.get(id(sem))
            if owner == eng.name and (eng.name == "pe" or not self.SAME_ENG_SYNC):
                continue
            if eng.waited.get(id(sem), 0) < val:
                eng.waited[id(sem)] = val
                waits.append((sem, val))
        return waits

    def _record(self, token, reads, writes, partial):
        sem, val = token
        for b in reads:
            if b.readers.get(sem, 0) < val:
                b.readers[sem] = val
        for b in writes:
            if b.writers.get(sem, 0) < val:
                b.writers[sem] = val

    def op(self, engname, fn, reads=(), writes=(), signal=True, partial=False, late_reads=(), late_writes=()):
        eng = self.engs[engname]
        waits = self._waits(eng, self._deps(reads, writes, partial))
        lwaits = []
        if late_reads or late_writes:
            lwaits = self._waits(eng, self._deps(late_reads, late_writes, partial))
            reads = list(reads) + list(late_reads)
            writes = list(writes) + list(late_writes)
            waits = waits + lwaits[:-1]
            lwaits = lwaits[-1:]
        token = None
        if signal:
            idx = eng.count // self.EPOCH
            while len(eng.sems) <= idx:
                s = self.nc.alloc_semaphore(f"c_{eng.name}_{len(eng.sems)}")
                self.semkeys[id(s)] = eng.name
                eng.sems.append(s)
            sem = eng.sems[idx]
            val = eng.count % self.EPOCH + 1
            eng.count += 1
            token = (sem, val)
            self._record(token, reads, writes, partial)
        eng.ops.append((waits, fn, token, 1, lwaits))
        return token

    def dma(self, queue, out, in_, reads=(), writes=(), partial=False, **kw):
        eng = self.engs[queue]
        if queue not in self.dma_slots:
            self.dma_slots[queue] = []
            self.dma_rr[queue] = 0
        slots = self.dma_slots[queue]
        rr = self.dma_rr[queue]
        self.dma_rr[queue] = rr + 1
        if len(slots) < self.n_dma_slots:
            s = self.nc.alloc_semaphore(f"d_{queue}_{len(slots)}")
            self.semkeys[id(s)] = "dma"
            slots.append([s, 0])
        slot = slots[rr % self.n_dma_slots]
        deps = self._deps(reads, writes, partial)
        if slot[1] > 0:
            self._merge(deps, {slot[0]: 16 * slot[1]})
        waits = self._waits(eng, deps)
        slot[1] += 1
        token = (slot[0], 16 * slot[1])
        self._record(token, reads, writes, partial)

        def fn(e, out=out, in_=in_, kw=kw):
            return e.dma_start(out=out, in_=in_, **kw)

        eng.ops.append((waits, fn, token, 16, []))
        return token

    def finish(self):
        eng = self.engs["sp"]
        deps = {}
        for q, slots in self.dma_slots.items():
            for s, n in slots:
                if n:
                    deps[s] = 16 * n
        for n, e in self.engs.items():
            if e.count:
                idx = (e.count - 1) // self.EPOCH
                deps[e.sems[idx]] = (e.count - 1) % self.EPOCH + 1
        waits = self._waits(eng, deps)
        eng.ops.append((waits, None, None, 0, []))

    def emit(self):
        with self.nc.Block() as block:
            def mk(engname):
                ops = self.engs[engname].ops

                def body(e):
                    for waits, fn, token, inc, lwaits in ops:
                        for sem, val in waits:
                            e.wait_ge(sem, val)
                        if fn is None:
                            continue
                        inst = fn(e)
                        for sem, val in lwaits:
                            inst._wait_ge(sem, val)
                        if token is not None:
                            inst.then_inc(token[0], inc)
                return body

            block.tensor(mk("pe"))
            block.scalar(mk("act"))
            block.vector(mk("dve"))
            block.gpsimd(mk("pool"))
            block.sync(mk("sp"))

from contextlib import ExitStack

NLAT = 4096
NCTX = 256
NTOK = NLAT + NCTX
D = 2048
EPS = 1e-6
ALPHA_C = 4 ** 0.25
NTB = 9


def tb_info(tb):
    return (tb * 512, 512 if tb < 8 else 256)


def _barrier(P):
    deps = {}
    for q, slots in P.dma_slots.items():
        for s, n in slots:
            if n:
                deps[s] = 16 * n
    for n, e in P.engs.items():
        if e.count:
            idx = (e.count - 1) // P.EPOCH
            deps[e.sems[idx]] = (e.count - 1) % P.EPOCH + 1
    for n, e in P.engs.items():
        waits = []
        for sem, val in deps.items():
            if e.waited.get(id(sem), 0) < val:
                e.waited[id(sem)] = val
                waits.append((sem, val))
        e.ops.append((waits, None, None, 0, []))


def _flush(P):
    _barrier(P)
    P.emit()
    for e in P.engs.values():
        e.ops = []


class Stage:
    uid = 0
    sid = 0
    def __init__(self, nc, P):
        self.nc = nc
        self.P = P
        self.es = ExitStack()
        self.n = 0

    def sb(self, shape, dt, name=None):
        Stage.uid += 1
        t = self.es.enter_context(self.nc.sbuf_tensor(name or f"t{Stage.uid}", shape, dt))
        return t, Buf()

    def ps(self, shape, dt=F32, name=None):
        Stage.uid += 1
        t = self.es.enter_context(self.nc.psum_tensor(name or f"p{Stage.uid}", shape, dt))
        return t, Buf()

    def __enter__(self):
        return self

    def __exit__(self, *a):
        Stage.sid += 1
        with self.nc.named_scope(f"stage{Stage.sid}"):
            _flush(self.P)
        self.es.close()
        return False


def stage_modvec(nc, P, T, layer):
    with Stage(nc, P) as S:
        for _ in body_modvec(S, nc, P, T, layer, "sp"):
            pass


def body_modvec(S, nc, P, T, layer, dmaq):
    if True:
        cs, b_cs = S.sb([128, 16, 2], F32)
        s, b_s = S.sb([128, 16, 2], F32)
        wb = [S.sb([128, 3072], F32) for _ in range(2)]
        ps, b_ps = S.ps([2, 3072], F32)
        row, b_row = S.sb([2, 6144], F32)
        bm, b_bm = S.sb([2, 6144], F32)
        pst, b_pst = S.ps([128, 96], F32)
        mod, b_mod = T["mod_fm"][layer]
        ident, b_id = T["ident"]
        P.dma(dmaq, cs[:], T["cs"], writes=[b_cs])
        P.dma(dmaq, bm[:], T["b_mod"][layer:layer + 1, :].partition_broadcast(2), writes=[b_bm])
        P.op("act", lambda e: e.activation(out=s[:], in_=cs[:], func=AF.Silu), reads=[b_cs], writes=[b_s])
        i = 0
        for half in range(2):
            for k in range(16):
                w, b_w = wb[i % 2]
                i += 1
                P.dma(dmaq, w[:], T["w_mod"][layer, k * 128:(k + 1) * 128, half * 3072:(half + 1) * 3072], writes=[b_w])
                for j in range(6):
                    last = (j == 5)
                    P.op("pe", lambda e, w=w, j=j, k=k: e.matmul(ps[:, j * 512:(j + 1) * 512], lhsT=s[:, k, :], rhs=w[:, j * 512:(j + 1) * 512], start=(k == 0), stop=(k == 15)),
                         reads=[b_s, b_w], writes=[b_ps], signal=last, partial=(k > 0))
            if half == 1:
                yield
            P.op("dve", lambda e, half=half: e.tensor_tensor(out=row[:, half * 3072:(half + 1) * 3072], in0=ps[:], in1=bm[:, half * 3072:(half + 1) * 3072], op=ALU.add),
                 reads=[b_ps, b_bm], writes=[b_row], partial=(half > 0))
        P.dma("sp", T["modrow"][layer], row[:], reads=[b_row], writes=[T["b_modrow"][layer]])
        for j in range(48):
            P.op("pe", lambda e, j=j: e.transpose(pst[:, j * 2:(j + 1) * 2], row[0:2, j * 128:(j + 1) * 128], ident[0:2, 0:2]),
                 reads=[b_row, b_id], writes=[b_pst], signal=(j == 47), partial=(j > 0))
        P.op("dve", lambda e: e.tensor_copy(out=mod[:], in_=pst[:]), reads=[b_pst], writes=[b_mod])
        P.op("dve", lambda e: e.tensor_scalar_add(out=mod[:, 32:64], in0=mod[:, 32:64], scalar1=1.0), reads=[b_mod], writes=[b_mod])


def stage_xinT(nc, P, T, layer):
    with Stage(nc, P) as S:
        xt = [[S.sb([128, 2048], F32) for _ in range(4)] for _ in range(3)]
        xo = [S.sb([128, 16, 512], BF16) for _ in range(3)]
        pp = [S.ps([128, 512], F32) for _ in range(4)]
        mod, b_mod = T["mod_fm"][layer]
        ident, b_id = T["ident"]
        k = 0
        for tb in range(NTB):
            t0, n = tb_info(tb)
            nt = n // 128
            r = 0 if tb < 8 else 1
            st = tb % 3
            for tt in range(nt):
                tile, b_t = xt[st][tt]
                row0 = t0 + tt * 128
                if layer == 0:
                    src = T["x"][row0:row0 + 128, :] if tb < 8 else T["ctx"][tt * 128:(tt + 1) * 128, :]
                    P.dma("sp", tile[:], src, writes=[b_t])
                else:
                    P.dma("sp", tile[:], T["xres"][row0:row0 + 128, :], reads=[T["b_xres"]], writes=[b_t])
            o, b_o = xo[st]
            for c in range(16):
                p_, b_p = pp[k % 4]
                k += 1
                for tt in range(nt):
                    tile, b_t = xt[st][tt]
                    P.op("pe", lambda e, p_=p_, tile=tile, tt=tt, c=c: e.transpose(p_[:, tt * 128:(tt + 1) * 128], tile[:, c * 128:(c + 1) * 128], ident[:]),
                         reads=[b_t, b_id], writes=[b_p], signal=(tt == nt - 1), partial=(tt > 0))
                sc_ap = mod[:, (16 + c) * 2 + r:(16 + c) * 2 + r + 1]
                bi_ap = mod[:, c * 2 + r:c * 2 + r + 1]
                if c % 2 == 0:
                    P.op("act", lambda e, o=o, p_=p_, c=c, n=n, sc_ap=sc_ap, bi_ap=bi_ap: e.activation(out=o[:, c, 0:n], in_=p_[:, 0:n], func=AF.Identity, scale=sc_ap, bias=bi_ap),
                         reads=[b_p, b_mod], writes=[b_o], partial=(c > 0))
                else:
                    P.op("dve", lambda e, o=o, p_=p_, c=c, n=n, sc_ap=sc_ap, bi_ap=bi_ap: e.tensor_scalar(out=o[:, c, 0:n], in0=p_[:, 0:n], scalar1=sc_ap, scalar2=bi_ap, op0=ALU.mult, op1=ALU.add),
                         reads=[b_p, b_mod], writes=[b_o], partial=(c > 0))
            P.dma("sp", T["xinT"][tb][:, :, 0:n], o[:, :, 0:n], reads=[b_o], writes=[T["b_xinT"]], partial=True)


def rope_tile(P, S, src, b_src, n, t0, tabs, perm, pm, out, b_out, wk):
    (Ct, b_C), (St, b_S) = tabs
    pmt, b_pm = pm
    (t1, b_t1), (t2, b_t2) = wk
    permt, b_perm = perm
    P.op("pe", lambda e: e.matmul(pmt[:, 0:n], lhsT=permt[:], rhs=src[:, 0:n], start=True, stop=True), reads=[b_src, b_perm], writes=[b_pm])
    P.op("pool", lambda e: e.tensor_tensor(out=t1[:, 0:n], in0=src[:, 0:n], in1=Ct[:, t0:t0 + n], op=ALU.mult), reads=[b_src, b_C], writes=[b_t1])
    P.op("dve", lambda e: e.tensor_tensor(out=t2[:, 0:n], in0=pmt[:, 0:n], in1=St[:, t0:t0 + n], op=ALU.mult), reads=[b_pm, b_S], writes=[b_t2])
    P.op("pool", lambda e: e.tensor_tensor(out=out[:, 0:n], in0=t1[:, 0:n], in1=t2[:, 0:n], op=ALU.add), reads=[b_t1, b_t2], writes=[b_out])


def stage_proj(nc, P, T, layer):
    if layer == 0:
        w_in = T["w_in0"]
        groups = [
            (0, 1024, [("store", j, ("featA", j)) for j in range(8)]),
            (1024, 1024, [("store", j, ("featA", 8 + j)) for j in range(8)]),
            (2048, 1024, [("store", j, ("featA", 16 + j)) for j in range(8)]),
            (3072, 1152, [("store32", j, ("latT", j)) for j in range(9)]),
            (4224, 1024, [("silu", j, ("gateT", j)) for j in range(8)]),
            (5248, 1024, [("silu", j, ("gateT", 8 + j)) for j in range(8)]),
        ]
    else:
        w_in = T["w_in1"]
        groups = [
            (0, 1024, [("qk", j, ("featA", j), 0) for j in range(8)]),
            (1024, 1024, [("qk", j, ("featA", 8 + j), 0) for j in range(8)]),
            (2048, 1024, [("qk", j, ("featA", 16 + j), 1) for j in range(4)] + [("v", 4, None)]),
            (3072, 1024, [("silu", j, ("gateT", j)) for j in range(8)]),
            (4096, 1024, [("silu", j, ("gateT", 8 + j)) for j in range(8)]),
        ]
    wv = w_in.rearrange("(c p) n -> p c n", p=128)
    with Stage(nc, P) as S:
        wt = [S.sb([128, 16, 1152], BF16) for _ in range(2)]
        xb = [S.sb([128, 16, 512], BF16) for _ in range(2)]
        NQ = 3
        mm = [S.ps([128, 512], F32) for _ in range(4)]
        ob = [S.sb([128, 512], BF16) for _ in range(6)]
        ob32 = [S.sb([128, 512], F32) for _ in range(2)]
        if layer == 1:
            ssp = [S.ps([128, 512], F32) for _ in range(2)]
            pmp = [S.ps([128, 512], F32) for _ in range(2)]
            sq = [S.sb([128, 512], BF16) for _ in range(NQ)]
            rstd = [S.sb([128, 512], F32) for _ in range(NQ)]
            qn = [S.sb([128, 512], F32) for _ in range(NQ)]
            wk = [(S.sb([128, 512], F32), S.sb([128, 512], F32)) for _ in range(NQ)]
            Ct = S.sb([128, NTOK], F32)
            St = S.sb([128, NTOK], F32)
            P.dma("sp", Ct[0][:], T["rope1"][0], writes=[Ct[1]])
            P.dma("sp", St[0][:], T["rope1"][1], writes=[St[1]])
            qkn, b_qkn = S.sb([128, 2], F32)
            P.dma("sp", qkn[:], T["qk_norm1"], writes=[b_qkn])
            vo = [S.sb([128, 512], BF16) for _ in range(2)]
        onesm, b_onesm = T["ones128"]
        cnt = dict(mm=0, ob=0, ob32=0, x=0, q=0, vo=0)

        def load_w(gi):
            col0, ncols, _ = groups[gi]
            w, b_w = wt[gi % 2]
            for c4 in range(4):
                P.dma("pool", w[:, c4 * 4:(c4 + 1) * 4, 0:ncols], wv[:, c4 * 4:(c4 + 1) * 4, col0:col0 + ncols], writes=[b_w], partial=(c4 > 0))

        seq = [(gi, tb) for gi in range(len(groups)) for tb in range(NTB)]

        def load_x(i):
            gi, tb = seq[i]
            t0, n = tb_info(tb)
            x_, b_x = xb[i % 2]
            P.dma("sp", x_[:, :, 0:n], T["xinT"][tb][:, :, 0:n], reads=[T["b_xinT"]], writes=[b_x])

        def epilogue(item, p_, b_p, n, t0):
            kind = item[0]
            dname, didx = item[2]
            dst = T[dname][didx][:, t0:t0 + n]
            b_dst = T["b_" + dname]
            if kind == "store32":
                o_, b_o = ob32[cnt["ob32"] % 2]
                cnt["ob32"] += 1
                P.op("dve", lambda e: e.tensor_copy(out=o_[:, 0:n], in_=p_[:, 0:n]), reads=[b_p], writes=[b_o])
                P.dma("sp", dst, o_[:, 0:n], reads=[b_o], writes=[b_dst], partial=True)
                return
            o_, b_o = ob[cnt["ob"] % len(ob)]
            cnt["ob"] += 1
            if kind == "store":
                if cnt["ob"] % 2:
                    P.op("act", lambda e: e.copy(out=o_[:, 0:n], in_=p_[:, 0:n]), reads=[b_p], writes=[b_o])
                else:
                    P.op("dve", lambda e: e.tensor_copy(out=o_[:, 0:n], in_=p_[:, 0:n]), reads=[b_p], writes=[b_o])
            elif kind == "silu":
                P.op("act", lambda e: e.activation(out=o_[:, 0:n], in_=p_[:, 0:n], func=AF.Silu), reads=[b_p], writes=[b_o])
            elif kind == "qk":
                which = item[3]
                q = cnt["q"] % NQ
                cnt["q"] += 1
                sq_, b_sq = sq[q]
                ss_, b_ss = ssp[q % 2]
                rs_, b_rs = rstd[q]
                qn_, b_qn = qn[q]
                pmt, b_pm = pmp[q % 2]
                (t1, b_t1), (t2, b_t2) = wk[q]
                (Cx, b_C), (Sx, b_S) = Ct, St
                permt, b_perm = T["perm1"]
                P.op("act", lambda e: e.activation(out=sq_[:, 0:n], in_=p_[:, 0:n], func=AF.Square), reads=[b_p], writes=[b_sq])
                yield
                P.op("pe", lambda e: e.matmul(ss_[:, 0:n], lhsT=onesm[:], rhs=sq_[:, 0:n], start=True, stop=True), reads=[b_sq, b_onesm], writes=[b_ss])
                P.op("act", lambda e: e.activation(out=rs_[:, 0:n], in_=ss_[:, 0:n], func=AF.Ln, bias=EPS), reads=[b_ss], writes=[b_rs])
                P.op("act", lambda e: e.activation(out=rs_[:, 0:n], in_=rs_[:, 0:n], func=AF.Exp, scale=-0.5), reads=[b_rs], writes=[b_rs])
                P.op("dve", lambda e: e.scalar_tensor_tensor(out=qn_[:, 0:n], in0=p_[:, 0:n], scalar=qkn[:, which:which + 1], in1=rs_[:, 0:n], op0=ALU.mult, op1=ALU.mult),
                     reads=[b_p, b_rs, b_qkn], writes=[b_qn])
                yield
                P.op("pe", lambda e: e.matmul(pmt[:, 0:n], lhsT=permt[:], rhs=qn_[:, 0:n], start=True, stop=True), reads=[b_qn, b_perm], writes=[b_pm])
                P.op("pool", lambda e: e.tensor_tensor(out=t1[:, 0:n], in0=qn_[:, 0:n], in1=Cx[:, t0:t0 + n], op=ALU.mult), reads=[b_qn, b_C], writes=[b_t1])
                P.op("dve", lambda e: e.tensor_tensor(out=t2[:, 0:n], in0=pmt[:, 0:n], in1=Sx[:, t0:t0 + n], op=ALU.mult), reads=[b_pm, b_S], writes=[b_t2])
                P.op("dve", lambda e: e.tensor_tensor(out=o_[:, 0:n], in0=t1[:, 0:n], in1=t2[:, 0:n], op=ALU.add), reads=[b_t1, b_t2], writes=[b_o])
            P.dma("sp", dst, o_[:, 0:n], reads=[b_o], writes=[b_dst], partial=True)

        pending = []

        def step_pending():
            for gen in list(pending):
                try:
                    next(gen)
                except StopIteration:
                    pending.remove(gen)

        load_w(0)
        load_x(0)
        for i, (gi, tb) in enumerate(seq):
            col0, ncols, items = groups[gi]
            if tb == 0 and gi + 1 < len(groups):
                load_w(gi + 1)
            if i + 1 < len(seq):
                load_x(i + 1)
            t0, n = tb_info(tb)
            w, b_w = wt[gi % 2]
            x_, b_x = xb[i % 2]
            for item in items:
                kind, j = item[0], item[1]
                if kind == "v":
                    for tt in range(n // 128):
                        p_, b_p = mm[cnt["mm"] % len(mm)]
                        cnt["mm"] += 1
                        for c in range(16):
                            P.op("pe", lambda e, p_=p_, x_=x_, w=w, c=c, tt=tt, j=j: e.matmul(p_[:, 0:512], lhsT=x_[:, c, tt * 128:(tt + 1) * 128], rhs=w[:, c, j * 128:j * 128 + 512], start=(c == 0), stop=(c == 15)),
                                 reads=[b_x, b_w], writes=[b_p], signal=(c == 15), partial=(c > 0))
                        step_pending()
                        v_, b_v = vo[cnt["vo"] % 2]
                        cnt["vo"] += 1
                        P.op("dve", lambda e, v_=v_, p_=p_: e.tensor_copy(out=v_[:], in_=p_[:]), reads=[b_p], writes=[b_v])
                        r0 = t0 + tt * 128
                        P.dma("sp", T["vtok"][r0:r0 + 128, 0:512], v_[:], reads=[b_v], writes=[T["b_vtok"]], partial=True)
                    continue
                p_, b_p = mm[cnt["mm"] % len(mm)]
                cnt["mm"] += 1
                for c in range(16):
                    P.op("pe", lambda e, p_=p_, x_=x_, w=w, c=c, j=j, n=n: e.matmul(p_[:, 0:n], lhsT=w[:, c, j * 128:(j + 1) * 128], rhs=x_[:, c, 0:n], start=(c == 0), stop=(c == 15)),
                         reads=[b_x, b_w], writes=[b_p], signal=(c == 15), partial=(c > 0))
                step_pending()
                gen = epilogue(item, p_, b_p, n, t0)
                pending.append(gen)
                try:
                    next(gen)
                except StopIteration:
                    pending.remove(gen)
        while pending:
            step_pending()


def stage_conv(nc, P, T, with_mod=None):
    with Stage(nc, P) as S:
        g = None
        if with_mod is not None:
            g = body_modvec(S, nc, P, T, with_mod, "act")
            next(g)
        body_conv(S, nc, P, T)
        if g is not None:
            for _ in g:
                pass


def body_conv(S, nc, P, T):
    if True:
        cw, b_cw = S.sb([128, 8, 3], F32)
        P.dma("sp", cw[:], T["conv_w"], writes=[b_cw])
        NB = 4
        cbt = [S.sb([128, 512], BF16) for _ in range(NB)]
        cct = [S.sb([128, 514], BF16) for _ in range(NB)]
        cut = [S.sb([128, 514], BF16) for _ in range(NB)]
        gtt = [S.sb([128, 512], BF16) for _ in range(NB)]
        vt = [S.sb([128, 514], F32) for _ in range(NB)]
        at = [S.sb([128, 512], F32) for _ in range(NB)]
        ot = [S.sb([128, 512], BF16) for _ in range(NB)]
        rd = [T["b_featA"], T["b_gateT"]]
        tiles = [(g, tb) for g in range(8) for tb in range(NTB)]

        def info(i):
            g, tb = tiles[i]
            t0, n = tb_info(tb)
            first = (t0 == 0 or t0 == NLAT)
            last = (t0 + n == NLAT or t0 + n == NTOK)
            return g, t0, n, first, last, i % NB

        def phA(i):
            g, t0, n, first, last, k = info(i)
            lo = 1 if first else 0
            hi = n + 1 if last else n + 2
            (cb_, b_cb), (cc_, b_cc), (cu_, b_cu), (gt_, b_gt), (v_, b_v) = cbt[k], cct[k], cut[k], gtt[k], vt[k]
            P.dma("sp", cb_[:, 0:n], T["featA"][g][:, t0:t0 + n], reads=rd, writes=[b_cb])
            P.dma("sp", cc_[:, lo:hi], T["featA"][8 + g][:, t0 - 1 + lo:t0 - 1 + hi], reads=rd, writes=[b_cc])
            P.dma("sp", cu_[:, lo:hi], T["featA"][16 + g][:, t0 - 1 + lo:t0 - 1 + hi], reads=rd, writes=[b_cu])
            P.dma("sp", gt_[:, 0:n], T["gateT"][g][:, t0:t0 + n], reads=rd, writes=[b_gt])
            P.op("pool", lambda e: e.tensor_tensor(out=v_[:, lo:hi], in0=cc_[:, lo:hi], in1=cu_[:, lo:hi], op=ALU.mult), reads=[b_cc, b_cu], writes=[b_v])
            if first:
                P.op("pool", lambda e: e.memset(v_[:, 0:1], 0.0), reads=[], writes=[b_v])
            if last:
                P.op("pool", lambda e: e.memset(v_[:, n + 1:n + 2], 0.0), reads=[], writes=[b_v])

        def phB(i):
            g, t0, n, first, last, k = info(i)
            (v_, b_v), (a_, b_a) = vt[k], at[k]
            P.op("dve", lambda e: e.tensor_scalar_mul(out=a_[:, 0:n], in0=v_[:, 0:n], scalar1=cw[:, g, 0:1]), reads=[b_v, b_cw], writes=[b_a])
            P.op("dve", lambda e: e.scalar_tensor_tensor(out=a_[:, 0:n], in0=v_[:, 1:n + 1], scalar=cw[:, g, 1:2], in1=a_[:, 0:n], op0=ALU.mult, op1=ALU.add), reads=[b_v, b_cw, b_a], writes=[b_a])
            P.op("dve", lambda e: e.scalar_tensor_tensor(out=a_[:, 0:n], in0=v_[:, 2:n + 2], scalar=cw[:, g, 2:3], in1=a_[:, 0:n], op0=ALU.mult, op1=ALU.add), reads=[b_v, b_cw, b_a], writes=[b_a])

        def phC(i):
            g, t0, n, first, last, k = info(i)
            (cb_, b_cb), (gt_, b_gt), (a_, b_a), (o_, b_o) = cbt[k], gtt[k], at[k], ot[k]
            P.op("pool", lambda e: e.tensor_tensor(out=a_[:, 0:n], in0=a_[:, 0:n], in1=cb_[:, 0:n], op=ALU.mult), reads=[b_a, b_cb], writes=[b_a])
            P.op("pool", lambda e: e.tensor_tensor(out=o_[:, 0:n], in0=a_[:, 0:n], in1=gt_[:, 0:n], op=ALU.mult), reads=[b_a, b_gt], writes=[b_o])
            P.dma("sp", T["mixT"][g][:, t0:t0 + n], o_[:, 0:n], reads=[b_o], writes=[T["b_mixT"]], partial=True)

        NT = len(tiles)
        phA(0)
        phA(1)
        phB(0)
        for i in range(NT):
            if i + 2 < NT:
                phA(i + 2)
            if i + 1 < NT:
                phB(i + 1)
            phC(i)


def stage_mla(nc, P, T):
    with Stage(nc, P) as S:
        wqb, b_wqb = S.sb([128, 4, 2048], BF16)
        wkvb, b_wkvb = S.sb([128, 4, 2048], BF16)
        P.dma("pool", wqb[:], T["w_qb"].rearrange("(c p) n -> p c n", p=128), writes=[b_wqb])
        P.dma("pool", wkvb[:], T["w_kvb"].rearrange("(c p) n -> p c n", p=128), writes=[b_wkvb])
        nrm, b_nrm = S.sb([128, 2, 4], F32)
        P.dma("sp", nrm[:], T["norm0"], writes=[b_nrm])
        Ct = S.sb([128, NTOK], F32)
        St = S.sb([128, NTOK], F32)
        P.dma("sp", Ct[0][:], T["rope0"][0], writes=[Ct[1]])
        P.dma("sp", St[0][:], T["rope0"][1], writes=[St[1]])
        lat = [S.sb([128, 9, 512], F32) for _ in range(2)]
        sq = [S.sb([128, 512], BF16) for _ in range(2)]
        ssp = [S.ps([128, 512], F32) for _ in range(2)]
        rstd = [S.sb([128, 512], F32) for _ in range(2)]
        lnb = [S.sb([128, 4, 512], BF16) for _ in range(2)]
        mm = [S.ps([128, 512], F32) for _ in range(3)]
        pmp = [S.ps([128, 512], F32) for _ in range(2)]
        ob = [S.sb([128, 512], BF16) for _ in range(4)]
        o32 = [S.sb([128, 512], F32) for _ in range(2)]
        wk = [(S.sb([128, 512], F32), S.sb([128, 512], F32)) for _ in range(2)]
        ones512, b_ones = T["ones512"]
        cnt = dict(mm=0, ob=0, o32=0, q=0, ln=0)
        rdl = [T["b_latT"]]
        latv = T["latT_all"]

        def nxt(key, lst):
            k = cnt[key] % len(lst)
            cnt[key] += 1
            return lst[k]

        def norm_phase(tb, which):
            t0, n = tb_info(tb)
            l_, b_l = lat[tb % 2]
            if which == 0:
                for j in range(9):
                    P.dma("sp", l_[:, j, 0:n], T["latT"][j][:, t0:t0 + n], reads=rdl, writes=[b_l], partial=(j > 0))
            ss_, b_ss = nxt("q", ssp)
            for c in range(4):
                sq_, b_sq = sq[c % 2]
                P.op("act", lambda e, sq_=sq_, c=c: e.activation(out=sq_[:, 0:n], in_=l_[:, which * 4 + c, 0:n], func=AF.Square), reads=[b_l], writes=[b_sq])
                P.op("pe", lambda e, sq_=sq_, c=c: e.matmul(ss_[:, 0:n], lhsT=ones512[:], rhs=sq_[:, 0:n], start=(c == 0), stop=(c == 3)), reads=[b_sq, b_ones], writes=[b_ss], partial=(c > 0))
            rs_, b_rs = rstd[which]
            P.op("act", lambda e: e.activation(out=rs_[:, 0:n], in_=ss_[:, 0:n], func=AF.Ln, bias=EPS), reads=[b_ss], writes=[b_rs])
            P.op("act", lambda e: e.activation(out=rs_[:, 0:n], in_=rs_[:, 0:n], func=AF.Exp, scale=-0.5), reads=[b_rs], writes=[b_rs])
            ln_, b_ln = lnb[which]
            for c in range(4):
                P.op("dve", lambda e, c=c: e.scalar_tensor_tensor(out=ln_[:, c, 0:n], in0=l_[:, which * 4 + c, 0:n], scalar=nrm[:, which, c:c + 1], in1=rs_[:, 0:n], op0=ALU.mult, op1=ALU.mult),
                     reads=[b_l, b_rs, b_nrm], writes=[b_ln], partial=(c > 0))

        def mm_phase(tb, which):
            t0, n = tb_info(tb)
            l_, b_l = lat[tb % 2]
            ln_, b_ln = lnb[which]
            wmat, b_wm = (wqb, b_wqb) if which == 0 else (wkvb, b_wkvb)
            nblk = 16 if which == 0 else 8
            for jb in range(nblk):
                p_, b_p = nxt("mm", mm)
                for c in range(4):
                    P.op("pe", lambda e, p_=p_, c=c, jb=jb: e.matmul(p_[:, 0:n], lhsT=wmat[:, c, jb * 128:(jb + 1) * 128], rhs=ln_[:, c, 0:n], start=(c == 0), stop=(c == 3)),
                         reads=[b_ln, b_wm], writes=[b_p], signal=(c == 3), partial=(c > 0))
                o_, b_o = nxt("ob", ob)
                if which == 0 and jb >= 8:
                    q32, b_q32 = nxt("o32", o32)
                    P.op("act", lambda e, q32=q32, p_=p_: e.copy(out=q32[:, 0:n], in_=p_[:, 0:n]), reads=[b_p], writes=[b_q32])
                    k2 = cnt["o32"] % 2
                    rope_tile(P, S, q32, b_q32, n, t0, (Ct, St), T["perm0"], pmp[k2], o_, b_o, wk[k2])
                    dst = T["qkT"][jb][:, t0:t0 + n]
                else:
                    P.op("act", lambda e, o_=o_, p_=p_: e.copy(out=o_[:, 0:n], in_=p_[:, 0:n]), reads=[b_p], writes=[b_o])
                    dst = T["qkT"][jb if which == 0 else 16 + jb][:, t0:t0 + n]
                P.dma("sp", dst, o_[:, 0:n], reads=[b_o], writes=[T["b_qkT"]], partial=True)
            if which == 1:
                for tt in range(n // 128):
                    for hf in range(2):
                        p_, b_p = nxt("mm", mm)
                        for c in range(4):
                            P.op("pe", lambda e, p_=p_, c=c, tt=tt, hf=hf: e.matmul(p_[:, 0:512], lhsT=ln_[:, c, tt * 128:(tt + 1) * 128], rhs=wkvb[:, c, 1024 + hf * 512:1024 + (hf + 1) * 512], start=(c == 0), stop=(c == 3)),
                                 reads=[b_ln, b_wkvb], writes=[b_p], signal=(c == 3), partial=(c > 0))
                        o_, b_o = nxt("ob", ob)
                        P.op("dve", lambda e, o_=o_, p_=p_: e.tensor_copy(out=o_[:], in_=p_[:]), reads=[b_p], writes=[b_o])
                        r0 = t0 + tt * 128
                        P.dma("sp", T["vtok"][r0:r0 + 128, hf * 512:(hf + 1) * 512], o_[:], reads=[b_o], writes=[T["b_vtok"]], partial=True)
                kr32, b_kr = nxt("o32", o32)
                P.op("act", lambda e: e.copy(out=kr32[:, 0:n], in_=l_[:, 8, 0:n]), reads=[b_l], writes=[b_kr])
                k2 = cnt["o32"] % 2
                o_, b_o = nxt("ob", ob)
                rope_tile(P, S, kr32, b_kr, n, t0, (Ct, St), T["perm0"], pmp[k2], o_, b_o, wk[k2])
                P.dma("sp", T["qkT"][24][:, t0:t0 + n], o_[:, 0:n], reads=[b_o], writes=[T["b_qkT"]], partial=True)

        steps = [(tb, which) for tb in range(NTB) for which in range(2)]
        norm_phase(*steps[0])
        for si, st_ in enumerate(steps):
            if si + 1 < len(steps):
                norm_phase(*steps[si + 1])
            mm_phase(*st_)


def stage_attn(nc, P, T, layer):
    H = 8 if layer == 0 else 16
    G = 1 if layer == 0 else 4
    scale = (192 ** -0.5) if layer == 0 else (128 ** -0.5)
    NKT = NTOK // 128
    POOL_SLOTS = (2, 5, 8) if layer == 0 else (1, 3, 5, 7)
    with Stage(nc, P) as S:
        KT = [S.sb([128, NTOK], BF16) for _ in range(2)]
        V = [S.sb([128, NKT, 128], BF16) for _ in range(2)]
        QT = [S.sb([128, NTOK], BF16) for _ in range(2)]
        if layer == 0:
            KR, b_KR = S.sb([128, NTOK], BF16)
            QR = [S.sb([128, NTOK], BF16) for _ in range(2)]
            P.dma("sp", KR[:], T["qkT"][24], reads=[T["b_qkT"]], writes=[b_KR])
        gt = [S.sb([128, 512], BF16) for _ in range(2)]
        PT = [S.sb([128, 1024], BF16) for _ in range(8)]
        accd = [S.sb([128, 1024], F32) for _ in range(2)]
        accp = [S.sb([128, 1024], F32) for _ in range(2)]
        rinv = [S.sb([128, 512], F32) for _ in range(2)]
        rg = [S.sb([128, 512], F32) for _ in range(2)]
        mo = [S.sb([128, 512], BF16) for _ in range(2)]
        sps = [S.ps([128, 1024], F32) for _ in range(2)]
        Ops = [S.ps([128, 512], F32) for _ in range(2)]
        RS = S.ps([128, 512], F32)
        onesf, b_onesf = T["onesf"]
        cnt = dict(s=0, pt=0, qb=0)
        srcq = T["b_qkT"] if layer == 0 else T["b_featA"]
        vview = T["vtok"].rearrange("(t p) d -> p t d", p=128)
        tails = []

        def step_tails(drain_parity=None):
            for ent in list(tails):
                tq, gen = ent
                if drain_parity is not None:
                    if tq % 2 == drain_parity:
                        for _ in gen:
                            pass
                        tails.remove(ent)
                    continue
                try:
                    next(gen)
                except StopIteration:
                    tails.remove(ent)

        def tail(qi, q0, nq, two, usedpool, o_, b_o, ad_, b_ad, ap_, b_ap, g_, b_g, mblk):
            yield
            if usedpool:
                P.op("dve", lambda e: e.tensor_tensor(out=ad_[:, 0:2 * nq], in0=ad_[:, 0:2 * nq], in1=ap_[:, 0:2 * nq], op=ALU.add), reads=[b_ad, b_ap], writes=[b_ad])
            yield
            rs_, b_rs = RS
            nh = 2 if two else 1
            for hf in range(nh):
                P.op("pe", lambda e, hf=hf: e.matmul(rs_[:, 0:nq], lhsT=onesf[:], rhs=ad_[:, hf * nq:(hf + 1) * nq], start=(hf == 0), stop=(hf == nh - 1)), reads=[b_ad, b_onesf], writes=[b_rs], signal=(hf == nh - 1), partial=(hf > 0))
            yield
            ri_, b_ri = rinv[qi % 2]
            rg_, b_rg = rg[qi % 2]
            P.op("act", lambda e: e.activation(out=ri_[:, 0:nq], in_=rs_[:, 0:nq], func=AF.Ln), reads=[b_rs], writes=[b_ri])
            P.op("act", lambda e: e.activation(out=ri_[:, 0:nq], in_=ri_[:, 0:nq], func=AF.Exp, scale=-1.0), reads=[b_ri], writes=[b_ri])
            yield
            P.op("pool", lambda e: e.tensor_tensor(out=rg_[:, 0:nq], in0=ri_[:, 0:nq], in1=g_[:, 0:nq], op=ALU.mult), reads=[b_ri, b_g], writes=[b_rg])
            yield
            m_, b_m = mo[qi % 2]
            P.op("dve", lambda e: e.tensor_tensor(out=m_[:, 0:nq], in0=o_[:, 0:nq], in1=rg_[:, 0:nq], op=ALU.mult), reads=[b_o, b_rg], writes=[b_m])
            P.dma("sp", T["mixT"][mblk][:, q0:q0 + nq], m_[:, 0:nq], reads=[b_m], writes=[T["b_mixT"]], partial=True)

        def load_kv(hk):
            kt_, b_kt = KT[hk % 2]
            v_, b_v = V[hk % 2]
            if layer == 0:
                P.dma("sp", kt_[:], T["qkT"][16 + hk], reads=[srcq], writes=[b_kt])
            else:
                P.dma("sp", kt_[:], T["featA"][16 + hk], reads=[srcq], writes=[b_kt])
            voff = hk * 128
            P.dma("sp", v_[:], vview[:, :, voff:voff + 128], reads=[T["b_vtok"]], writes=[b_v])

        def load_q(h):
            qt_, b_qt = QT[h % 2]
            if layer == 0:
                P.dma("sp", qt_[:], T["qkT"][h], reads=[srcq], writes=[b_qt])
                qr_, b_qr = QR[h % 2]
                P.dma("sp", qr_[:], T["qkT"][8 + h], reads=[srcq], writes=[b_qr])
            else:
                P.dma("sp", qt_[:, 0:NLAT], T["featA"][h][:, 0:NLAT], reads=[srcq], writes=[b_qt])

        load_kv(0)
        load_q(0)
        for hk in range(H // G):
            kt_, b_kt = KT[hk % 2]
            v_, b_v = V[hk % 2]
            for g in range(G):
                h = hk * G + g
                if h + 1 < H:
                    if (h + 1) % G == 0:
                        load_kv((h + 1) // G)
                    load_q(h + 1)
                qt_, b_qt = QT[h % 2]
                if layer == 0:
                    qr_, b_qr = QR[h % 2]
                    hb = (h % 2) * 64
                    gblk = 8 + h
                    qblocks = [(qb * 512, 512, list(range(NKT))) for qb in range(8)] + [(NLAT, 256, [32, 33])]
                    KRt, qrt, b_extra, hbv = KR, qr_, [b_KR, b_qr], hb
                else:
                    gblk = h
                    qblocks = [(qb * 512, 512, list(range(NKT))) for qb in range(8)]
                    KRt, qrt, b_extra, hbv = None, None, [], 0
                for (q0, nq, kts) in qblocks:
                    qi = cnt["qb"]
                    cnt["qb"] += 1
                    step_tails(drain_parity=qi % 2)
                    g_, b_g = gt[qi % 2]
                    P.dma("sp", g_[:, 0:nq], T["gateT"][gblk][:, q0:q0 + nq], reads=[T["b_gateT"]], writes=[b_g])
                    o_, b_o = Ops[qi % 2]
                    ad_, b_ad = accd[qi % 2]
                    ap_, b_ap = accp[qi % 2]
                    pairs = [kts[i:i + 2] for i in range(0, len(kts), 2)]
                    npair = len(pairs)
                    sbase = cnt["s"]
                    used = {"dve": False, "pool": False}

                    def qk(i, pairs=pairs, sbase=sbase, kt_=kt_, qt_=qt_, q0=q0, nq=nq, KRt=KRt, qrt=qrt, hbv=hbv, b_kt=b_kt, b_qt=b_qt, b_extra=b_extra):
                        s_, b_s = sps[(sbase + i) % 2]
                        if layer == 0 and nq < 512:
                            for hf, kt in enumerate(pairs[i]):
                                lastm = (hf == len(pairs[i]) - 1)
                                P.op("pe", lambda e, s_=s_, kt=kt, hf=hf: e.matmul(s_[:, hf * nq:(hf + 1) * nq], lhsT=kt_[:, kt * 128:(kt + 1) * 128], rhs=qt_[:, q0:q0 + nq], start=True, stop=False),
                                     reads=[b_kt], late_reads=[b_qt], late_writes=[b_s], signal=False, partial=(hf > 0))
                                P.op("pe", lambda e, s_=s_, kt=kt, hf=hf: e.matmul(s_[:, hf * nq:(hf + 1) * nq], lhsT=KRt[hf * 64:(hf + 1) * 64, kt * 128:(kt + 1) * 128], rhs=qrt[hf * 64:(hf + 1) * 64, q0:q0 + nq], start=False, stop=True),
                                     reads=[b_kt, b_qt] + b_extra, writes=[b_s], signal=lastm, partial=True)
                        elif layer == 0:
                            for hf, kt in enumerate(pairs[i]):
                                P.op("pe", lambda e, s_=s_, kt=kt, hf=hf: e.matmul(s_[:, hf * nq:(hf + 1) * nq], lhsT=kt_[:, kt * 128:(kt + 1) * 128], rhs=qt_[:, q0:q0 + nq], start=True, stop=False),
                                     reads=[b_kt], late_reads=[b_qt], late_writes=[b_s], signal=False, partial=(hf > 0))
                            for hf, kt in enumerate(pairs[i]):
                                lastm = (hf == len(pairs[i]) - 1)
                                P.op("pe", lambda e, s_=s_, kt=kt, hf=hf: e.matmul(s_[:, hf * nq:(hf + 1) * nq], lhsT=KRt[hf * 64:(hf + 1) * 64, kt * 128:(kt + 1) * 128], rhs=qrt[hf * 64:(hf + 1) * 64, q0:q0 + nq], start=False, stop=True),
                                     reads=[b_kt, b_qt] + b_extra, writes=[b_s], signal=lastm, partial=True)
                        else:
                            for hf, kt in enumerate(pairs[i]):
                                lastm = (hf == len(pairs[i]) - 1)
                                P.op("pe", lambda e, s_=s_, kt=kt, hf=hf: e.matmul(s_[:, hf * nq:(hf + 1) * nq], lhsT=kt_[:, kt * 128:(kt + 1) * 128], rhs=qt_[:, q0:q0 + nq], start=True, stop=True),
                                     reads=[b_kt], late_reads=[b_qt], late_writes=[b_s], signal=lastm, partial=(hf > 0))

                    qk(0)
                    for i in range(npair):
                        if i + 1 < npair:
                            qk(i + 1)
                        s_, b_s = sps[(sbase + i) % 2]
                        p_, b_p = PT[cnt["pt"] % 8]
                        cnt["pt"] += 1
                        w = len(pairs[i]) * nq
                        P.op("act", lambda e, p_=p_, s_=s_, w=w: e.activation(out=p_[:, 0:w], in_=s_[:, 0:w], func=AF.Exp, scale=scale), reads=[b_s], writes=[b_p])
                        for hf, kt in enumerate(pairs[i]):
                            first = (i == 0 and hf == 0)
                            lastm = (hf == len(pairs[i]) - 1)
                            P.op("pe", lambda e, o_=o_, p_=p_, hf=hf, kt=kt, first=first, lastall=(i == npair - 1 and lastm), v_=v_, nq=nq: e.matmul(o_[:, 0:nq], lhsT=v_[:, kt, :], rhs=p_[:, hf * nq:(hf + 1) * nq], start=first, stop=lastall),
                                 reads=[b_v], late_reads=[b_p], late_writes=[b_o], signal=lastm, partial=not first)
                        if npair == 1:
                            P.op("dve", lambda e, ad_=ad_, p_=p_, w=w: e.tensor_copy(out=ad_[:, 0:w], in_=p_[:, 0:w]), reads=[b_p], writes=[b_ad])
                        elif i == 0:
                            p0_, b_p0 = p_, b_p
                        elif i == 1:
                            P.op("dve", lambda e, ad_=ad_, p_=p_, p0_=p0_, w=w: e.tensor_tensor(out=ad_[:, 0:w], in0=p0_[:, 0:w], in1=p_[:, 0:w], op=ALU.add), reads=[b_p, b_p0], writes=[b_ad])
                        else:
                            P.op("dve", lambda e, ad_=ad_, p_=p_, w=w: e.tensor_tensor(out=ad_[:, 0:w], in0=ad_[:, 0:w], in1=p_[:, 0:w], op=ALU.add), reads=[b_p, b_ad], writes=[b_ad])
                        step_tails()
                    cnt["s"] += npair
                    tails.append((qi, tail(qi, q0, nq, True, False, o_, b_o, ad_, b_ad, ap_, b_ap, g_, b_g, (8 + h) if layer == 0 else h)))
        while tails:
            step_tails()


def load_wo(nc, P, T, layer, es):
    wo = es.enter_context(nc.sbuf_tensor(f"wo{layer}", [128, 16, 2048], BF16))
    b_wo = Buf()
    wsrc = (T["w_out0"] if layer == 0 else T["w_out1"]).rearrange("(c p) n -> p c n", p=128)
    for c4 in range(4):
        P.dma("pool", wo[:, c4 * 4:(c4 + 1) * 4, :], wsrc[:, c4 * 4:(c4 + 1) * 4, :], writes=[b_wo], partial=(c4 > 0))
    return wo, b_wo


def stage_out(nc, P, T, layer, wo_pre):
    with Stage(nc, P) as S:
        wo, b_wo = wo_pre
        gbc = [S.sb([128, 2048], F32) for _ in range(2 if layer == 0 else 1)]
        for r in range(len(gbc)):
            P.dma("sp", gbc[r][0][:], T["modrow"][layer][r:r + 1, 4096:6144].partition_broadcast(128), reads=[T["b_modrow"][layer]], writes=[gbc[r][1]])
        lg, b_lg = S.sb([128, 2048], F32)
        lb, b_lb = S.sb([128, 2048], F32)
        P.dma("sp", lg[:], T["ln_g"][layer:layer + 1, :].partition_broadcast(128), writes=[b_lg])
        P.dma("sp", lb[:], T["ln_b"][layer:layer + 1, :].partition_broadcast(128), writes=[b_lb])
        mt = [S.sb([128, 16, 512], BF16) for _ in range(2)]
        xt = [S.sb([128, 2048], F32) for _ in range(2)]
        za = [S.sb([128, 2048], F32) for _ in range(2)]
        zb = [S.sb([128, 2048], F32) for _ in range(2)]
        junk, b_junk = S.sb([128, 2048], BF16)
        st = [S.sb([128, 8], F32) for _ in range(2)]
        yps = [[S.ps([128, 512], F32) for _ in range(4)] for _ in range(2)]
        ntb = NTB if layer == 0 else 8
        ti = 0
        tails = []

        def step_tails(drain_parity=None):
            for ent in list(tails):
                tq, gen = ent
                if drain_parity is not None:
                    if tq % 2 == drain_parity:
                        for _ in gen:
                            pass
                        tails.remove(ent)
                    continue
                try:
                    next(gen)
                except StopIteration:
                    tails.remove(ent)

        def tail(a_, b_a, b_, b_b, s_, b_s, row0):
            yield
            P.op("dve", lambda e: e.tensor_scalar_mul(out=s_[:, 2:4], in0=s_[:, 0:2], scalar1=1.0 / D), reads=[b_s], writes=[b_s])
            P.op("dve", lambda e: e.scalar_tensor_tensor(out=s_[:, 4:5], in0=s_[:, 2:3], scalar=s_[:, 2:3], in1=s_[:, 3:4], op0=ALU.mult, op1=ALU.subtract), reads=[b_s], writes=[b_s])
            yield
            P.op("act", lambda e: e.activation(out=s_[:, 5:6], in_=s_[:, 4:5], func=AF.Sqrt, bias=EPS, scale=-1.0), reads=[b_s], writes=[b_s])
            yield
            P.op("dve", lambda e: e.reciprocal(out=s_[:, 6:7], in_=s_[:, 5:6]), reads=[b_s], writes=[b_s])
            P.op("dve", lambda e: e.scalar_tensor_tensor(out=s_[:, 7:8], in0=s_[:, 2:3], scalar=-1.0, in1=s_[:, 6:7], op0=ALU.mult, op1=ALU.mult), reads=[b_s], writes=[b_s])
            yield
            P.op("act", lambda e: e.activation(out=b_[:], in_=a_[:], func=AF.Identity, scale=s_[:, 6:7], bias=s_[:, 7:8]), reads=[b_a, b_s], writes=[b_b])
            yield
            P.op("dve", lambda e: e.tensor_tensor(out=b_[:], in0=b_[:], in1=lg[:], op=ALU.mult), reads=[b_b, b_lg], writes=[b_b])
            P.op("dve", lambda e: e.tensor_tensor(out=b_[:], in0=b_[:], in1=lb[:], op=ALU.add), reads=[b_b, b_lb], writes=[b_b])
            if layer == 0:
                P.dma("sp", T["xres"][row0:row0 + 128, :], b_[:], reads=[b_b], writes=[T["b_xres"]], partial=True)
            else:
                P.dma("sp", T["out"][row0:row0 + 128, :], b_[:], reads=[b_b])

        def load_mix(tb):
            t0, n = tb_info(tb)
            m_, b_m = mt[tb % 2]
            for c in range(16):
                P.dma("sp", m_[:, c, 0:n], T["mixT"][c][:, t0:t0 + n], reads=[T["b_mixT"]], writes=[b_m], partial=(c > 0))

        load_mix(0)
        for tb in range(ntb):
            t0, n = tb_info(tb)
            r = 0 if tb < 8 else 1
            m_, b_m = mt[tb % 2]
            if tb + 1 < ntb:
                load_mix(tb + 1)
            for tt in range(n // 128):
                k = ti % 2
                step_tails(drain_parity=k)
                ti += 1
                row0 = t0 + tt * 128
                x_, b_x = xt[k]
                if layer == 0:
                    src = T["x"][row0:row0 + 128, :] if tb < 8 else T["ctx"][tt * 128:(tt + 1) * 128, :]
                    P.dma("sp", x_[:], src, writes=[b_x])
                else:
                    P.dma("sp", x_[:], T["xres"][row0:row0 + 128, :], reads=[T["b_xres"]], writes=[b_x])
                a_, b_a = za[k]
                b_, b_b = zb[k]
                s_, b_s = st[k]
                g_, b_g = gbc[r]
                for nb in range(4):
                    y_, b_y = yps[k][nb]
                    for c in range(16):
                        P.op("pe", lambda e, y_=y_, m_=m_, c=c, tt=tt, nb=nb: e.matmul(y_[:], lhsT=m_[:, c, tt * 128:(tt + 1) * 128], rhs=wo[:, c, nb * 512:(nb + 1) * 512], start=(c == 0), stop=(c == 15)),
                             reads=[b_m, b_wo], writes=[b_y], signal=(c == 15), partial=(c > 0))
                    P.op("dve", lambda e, a_=a_, y_=y_, g_=g_, nb=nb: e.tensor_tensor(out=a_[:, nb * 512:(nb + 1) * 512], in0=y_[:], in1=g_[:, nb * 512:(nb + 1) * 512], op=ALU.mult),
                         reads=[b_y, b_g], writes=[b_a], partial=(nb > 0))
                    step_tails()
                P.op("dve", lambda e, a_=a_, x_=x_, s_=s_: e.scalar_tensor_tensor(out=a_[:], in0=x_[:], scalar=ALPHA_C, in1=a_[:], op0=ALU.mult, op1=ALU.add, accum_out=s_[:, 0:1]),
                     reads=[b_x, b_a], writes=[b_a, b_s])
                step_tails()
                P.op("act", lambda e, a_=a_, s_=s_: e.activation(out=junk[:], in_=a_[:], func=AF.Square, accum_out=s_[:, 1:2]), reads=[b_a, b_s], writes=[b_junk, b_s])
                step_tails()
                tails.append((k, tail(a_, b_a, b_, b_b, s_, b_s, row0)))
        while tails:
            step_tails()


STAGES = ["mod0", "mod1", "xin0", "proj0", "conv0", "mla0", "ao0", "xin1", "proj1", "ao1"]


def build_nc(debug=False, stop_after=None):
    Stage.sid = 0
    nc = bass.Bass("TRN2", target_bir_lowering=False)
    T = {}

    def inp(name, shape):
        T[name] = nc.dram_tensor(name, shape, F32, kind="ExternalInput").ap()

    inp("x", [NLAT, D]); inp("ctx", [NCTX, D]); inp("cs", [128, 16, 2])
    inp("w_mod", [2, D, 6144]); inp("b_mod", [2, 6144]); inp("ln_g", [2, D]); inp("ln_b", [2, D])
    inp("w_in0", [D, 6272]); inp("conv_w", [128, 8, 3]); inp("norm0", [128, 2, 4])
    inp("w_qb", [512, 2048]); inp("w_kvb", [512, 2048]); inp("w_out0", [D, D])
    inp("w_in1", [D, 5120]); inp("qk_norm1", [128, 2]); inp("w_out1", [D, D])
    inp("c_ident", [128, 128]); inp("c_perm0", [128, 128]); inp("c_perm1", [128, 128])
    inp("rope0", [2, 128, NTOK]); inp("rope1", [2, 128, NTOK])
    T["out"] = nc.dram_tensor("out", [NLAT, D], F32, kind="ExternalOutput").ap()
    skind = "ExternalOutput" if debug else "Internal"

    def scr(name, shape, dt):
        T[name + "_all"] = nc.dram_tensor(name, shape, dt, kind=skind).ap()
        T[name] = T[name + "_all"]
        T["b_" + name] = Buf(name)

    scr("modrow", [2, 2, 6144], F32)
    T["b_modrow"] = [Buf(), Buf()]
    scr("xinT", [NTB, 128, 16, 512], BF16)
    scr("featA", [24, 128, NTOK], BF16)
    scr("gateT", [16, 128, NTOK], BF16)
    scr("latT", [9, 128, NTOK], F32)
    scr("qkT", [25, 128, NTOK], BF16)
    scr("vtok", [NTOK, 1024], BF16)
    scr("mixT", [16, 128, NTOK], BF16)
    scr("xres", [NTOK, D], F32)

    P = Prog(nc)
    with ExitStack() as es:
        def psb(name, shape, dt):
            return es.enter_context(nc.sbuf_tensor(name, shape, dt)), Buf(name)
        T["ident"] = psb("ident", [128, 128], F32)
        T["identb"] = psb("identb", [128, 128], BF16)
        T["perm0"] = psb("perm0", [128, 128], F32)
        T["perm1"] = psb("perm1", [128, 128], F32)
        T["ones128"] = psb("ones128", [128, 128], BF16)
        T["ones512"] = psb("ones512", [128, 128], BF16)
        T["onesf"] = psb("onesf", [128, 128], F32)
        T["mod_fm"] = [psb("modfm0", [128, 96], F32), psb("modfm1", [128, 96], F32)]
        P.dma("sp", T["ident"][0][:], T["c_ident"], writes=[T["ident"][1]])
        P.dma("pool", T["identb"][0][:], T["c_ident"], writes=[T["identb"][1]])
        P.dma("sp", T["perm0"][0][:], T["c_perm0"], writes=[T["perm0"][1]])
        P.dma("sp", T["perm1"][0][:], T["c_perm1"], writes=[T["perm1"][1]])
        P.op("pool", lambda e: e.memset(T["ones128"][0][:], 1.0 / 128), writes=[T["ones128"][1]])
        P.op("pool", lambda e: e.memset(T["ones512"][0][:], 1.0 / 512), writes=[T["ones512"][1]])
        P.op("pool", lambda e: e.memset(T["onesf"][0][:], 1.0), writes=[T["onesf"][1]])
        def attn_out(layer):
            with ExitStack() as es2:
                wo_pre = load_wo(nc, P, T, layer, es2)
                stage_attn(nc, P, T, layer)
                stage_out(nc, P, T, layer, wo_pre)

        fns = {
            "mod0": lambda: stage_modvec(nc, P, T, 0), "mod1": lambda: stage_modvec(nc, P, T, 1),
            "xin0": lambda: stage_xinT(nc, P, T, 0), "proj0": lambda: stage_proj(nc, P, T, 0),
            "conv0": lambda: stage_conv(nc, P, T), "mla0": lambda: stage_mla(nc, P, T),
            "ao0": lambda: attn_out(0),
            "xin1": lambda: stage_xinT(nc, P, T, 1), "proj1": lambda: stage_proj(nc, P, T, 1),
            "ao1": lambda: attn_out(1),
        }
        for sname in STAGES:
            fns[sname]()
            if stop_after == sname:
                break
        P.finish()
        _flush(P)
    return nc


def _rope_tables():
    f = np.float32
    pr = np.repeat(np.arange(64), 64).astype(f)
    pc = np.tile(np.arange(64), 64).astype(f)

    def tab(dper, nrep):
        half = dper // 2
        inv = (10000.0 ** (-np.arange(half, dtype=f) / f(half))).astype(f)
        C = np.ones((2 * dper, NTOK), f)
        Sg = np.zeros((2 * dper, NTOK), f)
        for blk, pos in enumerate((pr, pc)):
            ang = (pos[:, None] * inv[None, :]).astype(f)
            c = np.cos(ang).T.astype(f)
            s = np.sin(ang).T.astype(f)
            base = blk * dper
            C[base:base + half, :NLAT] = c
            C[base + half:base + dper, :NLAT] = c
            Sg[base:base + half, :NLAT] = -s
            Sg[base + half:base + dper, :NLAT] = s
        return np.stack([np.tile(C, (nrep, 1)), np.tile(Sg, (nrep, 1))]).astype(f)

    def perm(dper, nrep):
        half = dper // 2
        n = 2 * dper * nrep
        Pm = np.zeros((n, n), f)
        for d_ in range(n):
            i = d_ % dper
            partner = d_ + half if i < half else d_ - half
            Pm[partner, d_] = 1.0
        return Pm

    return tab(32, 2), tab(64, 1), perm(32, 2), perm(64, 1)


def make_in_maps(inputs):
    f = np.float32
    g = lambda k: np.asarray(inputs[k], dtype=f)
    x, c, ctx, c_ctx = g("x"), g("c"), g("ctx"), g("c_ctx")
    a_w_in = g("a_w_in")[0]
    w_in0 = np.ascontiguousarray(np.concatenate([a_w_in[:, :4160], a_w_in[:, 4096:4160], a_w_in[:, 4160:]], axis=1))
    conv_w = np.ascontiguousarray(g("a_conv_w")[0].reshape(3, 8, 128).transpose(2, 1, 0))
    norm0 = np.ascontiguousarray(np.stack([g("a_q_norm")[0].reshape(4, 128), g("a_kv_norm")[0].reshape(4, 128)]).transpose(2, 0, 1))
    wq = g("a_w_qb")[0].reshape(512, 8, 192)
    w_qb = np.ascontiguousarray(np.concatenate([wq[:, :, :128].reshape(512, 1024), np.concatenate([wq[:, :, 128:], wq[:, :, 128:]], axis=2).reshape(512, 1024)], axis=1))
    wk = g("a_w_kvb")[0].reshape(512, 8, 256)
    w_kvb = np.ascontiguousarray(np.concatenate([wk[:, :, :128].reshape(512, 1024), wk[:, :, 128:].reshape(512, 1024)], axis=1))
    qk_norm1 = np.ascontiguousarray(np.stack([g("c_q_norm")[0], g("c_k_norm")[0]], axis=1))
    rope0, rope1, perm0, perm1 = _rope_tables()
    shared = {
        "w_mod": g("w_mod"), "b_mod": g("b_mod"), "ln_g": g("ln_g"), "ln_b": g("ln_b"),
        "w_in0": w_in0, "conv_w": conv_w, "norm0": norm0, "w_qb": w_qb, "w_kvb": w_kvb,
        "w_out0": np.ascontiguousarray(g("a_w_out")[0]), "w_in1": np.ascontiguousarray(g("c_w_in")[0]),
        "qk_norm1": qk_norm1, "w_out1": np.ascontiguousarray(g("c_w_out")[0]),
        "c_ident": np.eye(128, dtype=f), "c_perm0": perm0, "c_perm1": perm1, "rope0": rope0, "rope1": rope1,
    }
    maps = []
    for b in range(x.shape[0]):
        cs = np.ascontiguousarray(np.stack([c[b].reshape(16, 128).T, c_ctx.reshape(16, 128).T], axis=2))
        m = dict(shared)
        m.update({"x": np.ascontiguousarray(x[b]), "ctx": np.ascontiguousarray(ctx[b]), "cs": cs})
        maps.append(m)
    return maps


def kernel(**inputs):
    maps = make_in_maps(inputs)
    nc = build_nc()
    res = run_bass_kernel_spmd(nc, maps, core_ids=list(range(len(maps))))
    return np.stack([np.asarray(r["out"], dtype=np.float32) for r in res.results], axis=0)
```
